# Optimizing a Trainium2 kernel written in Bass

```python
import math
import jax
import jax.numpy as jnp
from jax import lax
import numpy as np

D_MODEL = 1024
BATCH = 8
SEQ = 4096
DEPTH = 1

GRID_W = 64
CTX_LEN = 256
MIX_W = D_MODEL
GDN_W = MIX_W // 2
GDN_DK = 128
GDN_DV = 128
GDN_HEADS = GDN_W // GDN_DV
SSM_W = MIX_W - GDN_W
SSM_HEADDIM = 64
SSM_HEADS = SSM_W // SSM_HEADDIM
SSM_GROUPS = 2
SSM_STATE = 128
SHORT_CONV = 3
CHUNK = 64
D_FF = ((8 * D_MODEL // 3 + 127) // 128) * 128
FFN_CONV = 3
EPS = 1e-6

GDN_SPLITS = (GDN_HEADS * GDN_DK, GDN_HEADS * GDN_DK, GDN_HEADS * GDN_DV, GDN_W, 2 * GDN_HEADS, 2 * GDN_HEADS)
SSM_SPLITS = (SSM_W, SSM_W, SSM_GROUPS * SSM_STATE, SSM_GROUPS * SSM_STATE, 2 * SSM_HEADS)
IN_DIM = sum(GDN_SPLITS) + sum(SSM_SPLITS)
GDN_CONV_CH = 2 * GDN_HEADS * GDN_DK + GDN_HEADS * GDN_DV
SSM_CONV_CH = SSM_W + 2 * SSM_GROUPS * SSM_STATE

kernel_name = 'hybrid_gdn_ssd_convglu_dit'


def _split(t, sizes):
    idx = np.cumsum(sizes)[:-1].tolist()
    return jnp.split(t, idx, axis=-1)


def _flip(t):
    return t[:, ::-1]


def rmsnorm(x, g):
    x32 = x.astype(jnp.float32)
    y = x32 * lax.rsqrt(jnp.mean(x32 * x32, axis=-1, keepdims=True) + EPS)
    return (y * g.astype(jnp.float32)).astype(x.dtype)


def l2norm(t):
    t32 = t.astype(jnp.float32)
    return t32 * lax.rsqrt(jnp.sum(t32 * t32, axis=-1, keepdims=True) + EPS)


def modulate(h, shift, scale):
    return h * (1 + scale) + shift


def dwconv1d(u, w, b=None):
    C = u.shape[-1]
    K = w.shape[0]
    out = lax.conv_general_dilated(u, w.reshape(K, 1, C).astype(u.dtype), window_strides=(1,),
                                   padding=((K // 2, K // 2),), dimension_numbers=('NWC', 'WIO', 'NWC'),
                                   feature_group_count=C)
    if b is not None:
        out = out + b
    return out


def dwconv2d_grid(u, w, b):
    Bsz, L, F = u.shape
    rows = L // GRID_W
    img = u.reshape(Bsz, rows, GRID_W, F)
    out = lax.conv_general_dilated(img, w.reshape(FFN_CONV, FFN_CONV, 1, F).astype(u.dtype), window_strides=(1, 1),
                                   padding=((FFN_CONV // 2, FFN_CONV // 2), (FFN_CONV // 2, FFN_CONV // 2)),
                                   dimension_numbers=('NHWC', 'HWIO', 'NHWC'), feature_group_count=F)
    return out.reshape(Bsz, L, F) + b


def gated_delta_chunked(q, k, v, g, beta, S0):
    Bsz, L, H, DK = q.shape
    DV = v.shape[-1]
    NC = L // CHUNK

    def blocks(t):
        t = jnp.moveaxis(t.astype(jnp.float32), 2, 1)
        return t.reshape(Bsz, H, NC, CHUNK, *t.shape[3:])

    q, k, v, g, beta = (blocks(t) for t in (q, k, v, g, beta))
    gc = jnp.cumsum(g, axis=-1)
    pos = jnp.arange(CHUNK)
    lower = pos[:, None] >= pos[None, :]
    decay = jnp.exp(jnp.where(lower, gc[..., :, None] - gc[..., None, :], -jnp.inf))
    kb = k * beta[..., None]
    kk = jnp.einsum('bhnid,bhnjd->bhnij', kb, k) * decay
    rhs = jnp.concatenate([v * beta[..., None], kb * jnp.exp(gc)[..., None]], axis=-1)
    sol = lax.linalg.triangular_solve(kk, rhs, left_side=True, lower=True, unit_diagonal=True)
    u, w = sol[..., :DV], sol[..., DV:]
    qk = jnp.einsum('bhnid,bhnjd->bhnij', q, k) * decay
    q_dec = q * jnp.exp(gc)[..., None]
    k_dec = k * jnp.exp(gc[..., -1:] - gc)[..., None]
    g_last = jnp.exp(gc[..., -1])
    xs = tuple(jnp.moveaxis(t, 2, 0) for t in (u, w, qk, q_dec, k_dec, g_last))

    def step(S, inp):
        u_c, w_c, qk_c, qd_c, kd_c, gl_c = inp
        v_new = u_c - jnp.einsum('bhcd,bhde->bhce', w_c, S)
        o_c = jnp.einsum('bhcd,bhde->bhce', qd_c, S) + jnp.einsum('bhij,bhje->bhie', qk_c, v_new)
        S = S * gl_c[..., None, None] + jnp.einsum('bhcd,bhce->bhde', kd_c, v_new)
        return S, o_c

    S, o = lax.scan(step, S0.astype(jnp.float32), xs)
    o = jnp.moveaxis(o, 0, 2).reshape(Bsz, H, L, DV)
    return jnp.moveaxis(o, 1, 2), S


def ssd_chunked(x, dt, A, Bm, Cm, h0):
    Bsz, L, H, P = x.shape
    G, N = Bm.shape[2], Bm.shape[3]
    R = H // G
    NC = L // CHUNK
    f32 = jnp.float32
    xc = x.astype(f32).reshape(Bsz, NC, CHUNK, G, R, P)
    dtc = dt.astype(f32).reshape(Bsz, NC, CHUNK, G, R)
    Bc = Bm.astype(f32).reshape(Bsz, NC, CHUNK, G, N)
    Cc = Cm.astype(f32).reshape(Bsz, NC, CHUNK, G, N)
    cum = jnp.cumsum(dtc * A.astype(f32).reshape(G, R), axis=2)
    pos = jnp.arange(CHUNK)
    lower = (pos[:, None] >= pos[None, :])[:, :, None, None]
    seg = jnp.exp(jnp.where(lower, cum[:, :, :, None] - cum[:, :, None, :], -jnp.inf))
    cb = jnp.einsum('bnigs,bnjgs->bnijg', Cc, Bc)
    y_diag = jnp.einsum('bnijg,bnijgr,bnjgrp->bnigrp', cb, seg, dtc[..., None] * xc)
    xdt_end = (jnp.exp(cum[:, :, -1:] - cum) * dtc)[..., None] * xc
    states = jnp.einsum('bnjgs,bnjgrp->bngrps', Bc, xdt_end)
    chunk_decay = jnp.exp(cum[:, :, -1])

    def step(h, inp):
        st, dc = inp
        return h * dc[..., None, None] + st, h

    h_fin, h_in = lax.scan(step, h0.astype(f32).reshape(Bsz, G, R, P, N),
                           (jnp.moveaxis(states, 1, 0), jnp.moveaxis(chunk_decay, 1, 0)))
    h_in = jnp.moveaxis(h_in, 0, 1)
    y_off = jnp.einsum('bnigs,bngrps,bnigr->bnigrp', Cc, h_in, jnp.exp(cum))
    y = (y_diag + y_off).reshape(Bsz, L, H, P)
    return y, h_fin.reshape(Bsz, H, P, N)


def token_mixers(h, w_in, gdn_conv_w, gdn_A_log, gdn_dt_bias, gdn_norm_g, ssm_conv_w, ssm_conv_b,
                 ssm_A_log, ssm_dt_bias, ssm_D, ssm_norm_g, states, with_output):
    f32 = jnp.float32
    Bsz, L = h.shape[0], h.shape[1]
    proj = h @ w_in
    q, k, v, z_g, a_g, b_g, z_s, x_s, B_s, C_s, dt_s = _split(proj, GDN_SPLITS + SSM_SPLITS)
    S_gf0, S_gb0, S_sf0, S_sb0 = states

    qkv = jax.nn.silu(dwconv1d(jnp.concatenate([q, k, v], axis=-1), gdn_conv_w))
    q, k, v = _split(qkv, (GDN_HEADS * GDN_DK, GDN_HEADS * GDN_DK, GDN_HEADS * GDN_DV))
    q = l2norm(q.reshape(Bsz, L, GDN_HEADS, GDN_DK)) * (GDN_DK ** -0.5)
    k = l2norm(k.reshape(Bsz, L, GDN_HEADS, GDN_DK))
    v = v.reshape(Bsz, L, GDN_HEADS, GDN_DV)
    a = a_g.reshape(Bsz, L, 2, GDN_HEADS).astype(f32)
    g_log = -jnp.exp(gdn_A_log.astype(f32)) * jax.nn.softplus(a + gdn_dt_bias.astype(f32))
    beta = jax.nn.sigmoid(b_g.reshape(Bsz, L, 2, GDN_HEADS).astype(f32))
    o_f, S_gf = gated_delta_chunked(q, k, v, g_log[:, :, 0], beta[:, :, 0], S_gf0)
    o_b, S_gb = gated_delta_chunked(_flip(q), _flip(k), _flip(v), _flip(g_log[:, :, 1]), _flip(beta[:, :, 1]), S_gb0)

    xBC = jax.nn.silu(dwconv1d(jnp.concatenate([x_s, B_s, C_s], axis=-1), ssm_conv_w, ssm_conv_b))
    xs, Bm, Cm = _split(xBC, (SSM_W, SSM_GROUPS * SSM_STATE, SSM_GROUPS * SSM_STATE))
    xs = xs.reshape(Bsz, L, SSM_HEADS, SSM_HEADDIM)
    Bm = Bm.reshape(Bsz, L, SSM_GROUPS, SSM_STATE)
    Cm = Cm.reshape(Bsz, L, SSM_GROUPS, SSM_STATE)
    dt = jax.nn.softplus(dt_s.reshape(Bsz, L, 2, SSM_HEADS).astype(f32) + ssm_dt_bias.astype(f32))
    A = -jnp.exp(ssm_A_log.astype(f32))
    y_f, S_sf = ssd_chunked(xs, dt[:, :, 0], A[0], Bm, Cm, S_sf0)
    y_b, S_sb = ssd_chunked(_flip(xs), _flip(dt[:, :, 1]), A[1], _flip(Bm), _flip(Cm), S_sb0)
    new_states = (S_gf, S_gb, S_sf, S_sb)
    if not with_output:
        return None, new_states

    o = o_f + _flip(o_b)
    o = rmsnorm(o, gdn_norm_g) * jax.nn.silu(z_g.reshape(Bsz, L, GDN_HEADS, GDN_DV).astype(f32))
    o = o.reshape(Bsz, L, GDN_W)
    y = y_f + _flip(y_b) + ssm_D.astype(f32)[:, None] * xs.astype(f32)
    y = rmsnorm(y.reshape(Bsz, L, SSM_W) * jax.nn.silu(z_s.astype(f32)), ssm_norm_g)
    mix = jnp.concatenate([o, y.astype(f32)], axis=-1).astype(h.dtype)
    return mix, new_states


def conv_glu(h, w_gate, w_up, conv_w, conv_b, w_down, on_grid):
    a = h @ w_gate
    if on_grid:
        a = dwconv2d_grid(a, conv_w, conv_b)
    else:
        a = dwconv1d(a, conv_w[FFN_CONV // 2], conv_b)
    return (jax.nn.silu(a) * (h @ w_up)) @ w_down


def setup_inputs(seed: int = 0) -> dict:
    key = jax.random.key(seed)
    ks = jax.random.split(key, 32)
    f32 = jnp.float32

    def nrm(k, shape, scale):
        return jax.random.normal(k, shape, f32) * scale

    def gain(k, shape):
        return 1.0 + 0.05 * jax.random.normal(k, shape, f32)

    def a_log(k, shape):
        return jnp.log(jax.random.uniform(k, shape, f32, 1.0, 16.0))

    def dt_bias(k, shape):
        dtv = jnp.exp(jax.random.uniform(k, shape, f32, math.log(1e-3), math.log(1e-1)))
        return dtv + jnp.log(-jnp.expm1(-dtv))

    return {
        'x': nrm(ks[0], (BATCH, SEQ, D_MODEL), 1.0),
        'c': nrm(ks[1], (BATCH, D_MODEL), 1.0),
        'ctx': nrm(ks[2], (BATCH, CTX_LEN, D_MODEL), 1.0),
        'c_ctx': nrm(ks[3], (D_MODEL,), 1.0),
        'w_ada': nrm(ks[4], (DEPTH, D_MODEL, 6 * D_MODEL), 0.5 * D_MODEL ** -0.5),
        'b_ada': nrm(ks[5], (DEPTH, 6 * D_MODEL), 0.01),
        'norm1_g': gain(ks[6], (DEPTH, D_MODEL)),
        'w_in': nrm(ks[7], (DEPTH, D_MODEL, IN_DIM), D_MODEL ** -0.5),
        'gdn_conv_w': nrm(ks[8], (DEPTH, SHORT_CONV, GDN_CONV_CH), SHORT_CONV ** -0.5),
        'gdn_A_log': a_log(ks[9], (DEPTH, 2, GDN_HEADS)),
        'gdn_dt_bias': dt_bias(ks[10], (DEPTH, 2, GDN_HEADS)),
        'gdn_norm_g': gain(ks[11], (DEPTH, GDN_DV)),
        'ssm_conv_w': nrm(ks[12], (DEPTH, SHORT_CONV, SSM_CONV_CH), SHORT_CONV ** -0.5),
        'ssm_conv_b': nrm(ks[13], (DEPTH, SSM_CONV_CH), 0.02),
        'ssm_A_log': a_log(ks[14], (DEPTH, 2, SSM_HEADS)),
        'ssm_dt_bias': dt_bias(ks[15], (DEPTH, 2, SSM_HEADS)),
        'ssm_D': 1.0 + 0.1 * jax.random.normal(ks[16], (DEPTH, SSM_HEADS), f32),
        'ssm_norm_g': gain(ks[17], (DEPTH, SSM_W)),
        'w_out': nrm(ks[18], (DEPTH, MIX_W, D_MODEL), MIX_W ** -0.5),
        'norm2_g': gain(ks[19], (DEPTH, D_MODEL)),
        'ffn_w_gate': nrm(ks[20], (DEPTH, D_MODEL, D_FF), D_MODEL ** -0.5),
        'ffn_w_up': nrm(ks[21], (DEPTH, D_MODEL, D_FF), D_MODEL ** -0.5),
        'ffn_conv_w': nrm(ks[22], (DEPTH, FFN_CONV, FFN_CONV, D_FF), 1.0 / FFN_CONV),
        'ffn_conv_b': nrm(ks[23], (DEPTH, D_FF), 0.02),
        'ffn_w_down': nrm(ks[24], (DEPTH, D_FF, D_MODEL), D_FF ** -0.5),
        'final_norm_g': gain(ks[25], (D_MODEL,)),
    }


def reference(x, c, ctx, c_ctx, w_ada, b_ada, norm1_g, w_in, gdn_conv_w, gdn_A_log, gdn_dt_bias, gdn_norm_g,
              ssm_conv_w, ssm_conv_b, ssm_A_log, ssm_dt_bias, ssm_D, ssm_norm_g, w_out, norm2_g,
              ffn_w_gate, ffn_w_up, ffn_conv_w, ffn_conv_b, ffn_w_down, final_norm_g):
    f32 = jnp.float32
    Bsz = x.shape[0]
    for l in range(DEPTH):
        last = l == DEPTH - 1
        mod = jax.nn.silu(c) @ w_ada[l] + b_ada[l]
        mod_c = jax.nn.silu(c_ctx) @ w_ada[l] + b_ada[l]
        sh1, sc1, g1, sh2, sc2, g2 = jnp.split(mod[:, None, :], 6, axis=-1)
        sh1c, sc1c, g1c, sh2c, sc2c, g2c = jnp.split(mod_c, 6, axis=-1)
        mixer_params = (w_in[l], gdn_conv_w[l], gdn_A_log[l], gdn_dt_bias[l], gdn_norm_g[l], ssm_conv_w[l],
                        ssm_conv_b[l], ssm_A_log[l], ssm_dt_bias[l], ssm_D[l], ssm_norm_g[l])
        ffn_params = (ffn_w_gate[l], ffn_w_up[l], ffn_conv_w[l], ffn_conv_b[l], ffn_w_down[l])
        zero_states = (jnp.zeros((Bsz, GDN_HEADS, GDN_DK, GDN_DV), f32),
                       jnp.zeros((Bsz, GDN_HEADS, GDN_DK, GDN_DV), f32),
                       jnp.zeros((Bsz, SSM_HEADS, SSM_HEADDIM, SSM_STATE), f32),
                       jnp.zeros((Bsz, SSM_HEADS, SSM_HEADDIM, SSM_STATE), f32))
        hc = modulate(rmsnorm(ctx, norm1_g[l]), sh1c, sc1c)
        mix_c, ctx_states = token_mixers(hc, *mixer_params, zero_states, not last)
        hx = modulate(rmsnorm(x, norm1_g[l]), sh1, sc1)
        mix_x, _ = token_mixers(hx, *mixer_params, ctx_states, True)
        x = x + g1 * (mix_x @ w_out[l])
        x = x + g2 * conv_glu(modulate(rmsnorm(x, norm2_g[l]), sh2, sc2), *ffn_params, True)
        if not last:
            ctx = ctx + g1c * (mix_c @ w_out[l])
            ctx = ctx + g2c * conv_glu(modulate(rmsnorm(ctx, norm2_g[l]), sh2c, sc2c), *ffn_params, False)
    return rmsnorm(x, final_norm_g)
```

```python
import numpy as np
from contextlib import ExitStack
import concourse.bass as bass
import concourse.mybir as mybir
from concourse.bass_utils import run_bass_kernel_spmd

F32 = mybir.dt.float32
BF16 = mybir.dt.bfloat16
AF = mybir.ActivationFunctionType
ALU = mybir.AluOpType

D = 1024
KC = 8
L = 4096
LC = 256
LT = L + LC
NT = LT // 128
NTC = LC // 128
DFF = 2816
FC = DFF // 128
EPS = 1e-6
KD = 8

ENGS = ("pe", "dve", "act", "pool", "sp")


class Buf:
    __slots__ = ("name", "t", "last_w", "reads")

    def __init__(self, name, t):
        self.name = name
        self.t = t
        self.last_w = None
        self.reads = {}

    def __getitem__(self, k):
        return self.t[k]


class Sched:
    def __init__(self, nc, es):
        self.nc = nc
        self.es = es
        self.streams = {e: [] for e in ENGS}
        self.semh = {}
        self.cnt = {e: 0 for e in ENGS}
        self.seen = {e: {} for e in ENGS}
        for e in ("pe", "dve", "act", "pool"):
            self.semh[e] = es.enter_context(nc.semaphore("s_" + e))
        self.dma_n = {}
        for q in ("sp", "pool", "act"):
            self.dma_n[q] = 0
            for i in range(KD):
                nm = "d_%s%d" % (q, i)
                self.semh[nm] = es.enter_context(nc.semaphore(nm))
        self.n_ops = 0

    def sb(self, es, name, shape, dt):
        self.n_sb = getattr(self, "n_sb", 0) + 1
        name = "%s_u%d" % (name, self.n_sb)
        return Buf(name, es.enter_context(self.nc.sbuf_tensor(name, list(shape), dt)))

    def ps(self, es, name, shape, dt):
        return Buf(name, es.enter_context(self.nc.psum_tensor(name, list(shape), dt)))

    def dram(self, name, shape, dt, kind="Internal"):
        return Buf(name, self.nc.dram_tensor(name, list(shape), dt, kind=kind).ap())

    def _deps(self, reads, writes):
        deps = {}

        def add(k, v):
            if deps.get(k, 0) < v:
                deps[k] = v
        for b in reads:
            if b.last_w is not None:
                add(*b.last_w)
        for b in writes:
            if b.last_w is not None:
                add(*b.last_w)
            for k, v in b.reads.items():
                add(k, v)
        return deps

    def _waits(self, e, deps):
        for k, v in deps.items():
            if k == e and e == "pe":
                continue
            if self.seen[e].get(k, 0) >= v:
                continue
            self.streams[e].append(("wait", k, v))
            self.seen[e][k] = v

    def _record(self, ev, reads, writes):
        k, v = ev
        for b in reads:
            if b.reads.get(k, 0) < v:
                b.reads[k] = v
        for b in writes:
            b.last_w = ev
            b.reads = {}

    def op(self, e, fn, reads=(), writes=(), inc=True):
        self.n_ops += 1
        self._waits(e, self._deps(reads, writes))
        if inc:
            self.cnt[e] += 1
            ev = (e, self.cnt[e])
            self.streams[e].append(("op", fn, True))
        else:
            ev = (e, self.cnt[e] + 1)
            self.streams[e].append(("op", fn, False))
        self._record(ev, reads, writes)

    def dma(self, q, out_ap, in_ap, reads=(), writes=(), **kw):
        self.n_ops += 1
        n = self.dma_n[q]
        self.dma_n[q] += 1
        nm = "d_%s%d" % (q, n % KD)
        val = 16 * (n // KD + 1)
        deps = self._deps(reads, writes)
        if n >= KD and deps.get(nm, 0) < val - 16:
            deps[nm] = val - 16
        self._waits(q, deps)
        self.streams[q].append(("dma", out_ap, in_ap, nm, kw))
        self._record((nm, val), reads, writes)

    def barrier(self):
        tot = {}
        for e in ("pe", "dve", "act", "pool"):
            if self.cnt[e] > 0:
                tot[e] = self.cnt[e]
        for q in ("sp", "pool", "act"):
            n = self.dma_n[q]
            for i in range(min(n, KD)):
                cntd = (n - 1 - i) // KD + 1
                tot["d_%s%d" % (q, i)] = 16 * cntd
        for e in ENGS:
            d = dict(tot)
            self._waits_all(e, d)

    def _waits_all(self, e, deps):
        for k, v in deps.items():
            if self.seen[e].get(k, 0) >= v:
                continue
            self.streams[e].append(("wait", k, v))
            self.seen[e][k] = v

    def replay(self, e, eng):
        for it in self.streams[e]:
            if it[0] == "wait":
                eng.wait_ge(self.semh[it[1]], it[2])
            elif it[0] == "op":
                ins = it[1](eng)
                if it[2]:
                    ins.then_inc(self.semh[e], 1)
            else:
                _, o, i, nm, kw = it
                eng.dma_start(out=o, in_=i, **kw).then_inc(self.semh[nm], 16)


C_Q, C_K, C_V, C_ZG, C_A, C_B, C_ZS, C_XS, C_BS, C_CS, C_DT = 0, 512, 1024, 1536, 2048, 2056, 2064, 2576, 3088, 3344, 3600


def post_block(S, gname, ci, a16, t0, ntk, cnt, tmst, pb, identb, cstb, DR):
    fm_dst = None
    tm_dst = None
    if gname == "q":
        fm_dst = DR["QT"].t[ci, :, t0:t0 + ntk]; fmb = DR["QT"]
    elif gname == "k":
        fm_dst = DR["KT"].t[ci, :, t0:t0 + ntk]; fmb = DR["KT"]
        tm_dst = (DR["Ktm"], ci * 128)
    elif gname == "v":
        tm_dst = (DR["Vtm"], ci * 128)
    elif gname == "xs":
        tm_dst = (DR["Xtm"], ci * 128)
    elif gname == "bc":
        if ci < 2:
            fm_dst = DR["BT"].t[ci, :, t0:t0 + ntk]; fmb = DR["BT"]
            tm_dst = (DR["Btm"], ci * 128)
        else:
            fm_dst = DR["CT"].t[ci - 2, :, t0:t0 + ntk]; fmb = DR["CT"]
    if fm_dst is not None:
        S.dma("sp", fm_dst, a16.t[:, 0:ntk], reads=[a16], writes=[fmb])
    if tm_dst is not None:
        dbuf, c0 = tm_dst
        nsub = ntk // 128
        pt = pb[cnt["tm"] % 2]
        st = tmst[cnt["tm"] % 2]
        cnt["tm"] += 1
        for sidx in range(nsub):
            S.op("pe", lambda e, pt=pt, sidx=sidx, a16=a16: e.transpose(out=pt.t[:, sidx * 128:(sidx + 1) * 128],
                                                                       in_=a16.t[:, sidx * 128:(sidx + 1) * 128], identity=identb),
                 reads=[a16, cstb], writes=[pt], inc=(sidx == nsub - 1))
        S.op("dve", lambda e, pt=pt, st=st, nsub=nsub: e.tensor_copy(out=st.t[:, 0:nsub, :].rearrange("p a b -> p (a b)"), in_=pt.t[:, 0:nsub * 128]),
             reads=[pt], writes=[st])
        S.dma("sp", dbuf.t[t0:t0 + ntk, c0:c0 + 128].rearrange("(a p) c -> p a c", p=128), st.t[:, 0:nsub, :], reads=[st], writes=[dbuf])


import os as _os
GDN_ON = True
RUN3B = _os.environ.get('RUN3B', '1') == '1'
SSD_ON = True


def build(debug=None):
    nc = bass.Bass("TRN2", target_bir_lowering=False)
    dbg = {}

    def ext_in(name, shape, dt=F32):
        return Buf(name, nc.dram_tensor(name, list(shape), dt, kind="ExternalInput").ap())

    x_in = ext_in("x", [L, D])
    ctx_in = ext_in("ctx", [LC, D])
    cT_in = ext_in("cT", [128, KC, 2])
    wada_in = ext_in("w_ada", [D, 6 * D])
    bfm_in = ext_in("b_ada_fm", [128, 32])
    brow_in = ext_in("b_ada_row", [1, 2048])
    n1g_in = ext_in("n1g_fm", [128, KC])
    n2g_in = ext_in("n2g_fm", [128, KC])
    win_in = ext_in("w_in", [D, 3616])
    cw_in = ext_in("convw_fm", [128, 20, 3])
    cb_in = ext_in("convb_fm", [128, 20])
    cst_in = ext_in("consts", [128, 1024])
    gcb_in = ext_in("gate_bias", [128, NT, 24])
    gca_in = ext_in("gate_alog", [128, NT, 24])
    msk_in = ext_in("masks", [128, 32, 128])
    wout_in = ext_in("w_out", [D, D])
    nrm_in = ext_in("nrm", [128, 3, 512])
    wgate_in = ext_in("w_gate", [D, DFF])
    wup_in = ext_in("w_up", [D, DFF])
    wdown_in = ext_in("w_down", [DFF, D])
    fcw_in = ext_in("fcw_fm", [128, FC, 9])
    fcb_in = ext_in("fcb_fm", [128, FC])
    fng_in = ext_in("fng_bc", [128, D])
    y_out = Buf("y", nc.dram_tensor("y", [L, D], F32, kind="ExternalOutput").ap())

    with ExitStack() as es0:
        S = Sched(nc, es0)

        def dkind(name):
            return "ExternalOutput" if (debug and name in debug) else "Internal"

        def scratch(name, shape, dt):
            b = S.dram(name, shape, dt, kind=dkind(name))
            return b

        QT = scratch("QT", [4, 128, LT], BF16)
        KT = scratch("KT", [4, 128, LT], BF16)
        Ktm = scratch("Ktm", [LT, 512], BF16)
        Vtm = scratch("Vtm", [LT, 512], BF16)
        Xtm = scratch("Xtm", [LT, 512], BF16)
        BT = scratch("BT", [2, 128, LT], BF16)
        CT = scratch("CT", [2, 128, LT], BF16)
        Btm = scratch("Btm", [LT, 256], BF16)
        ZG = scratch("ZG", [L, 512], BF16)
        ZS = scratch("ZS", [L, 512], BF16)
        GCT = scratch("GCT", [NT * 24, 128], F32)
        UWQ = scratch("UWQ", [NT, 2, 4, 128, 512], BF16)
        OF = scratch("OF", [L, 512], F32)
        OB = scratch("OB", [L, 512], F32)
        YF = scratch("YF", [L, 512], F32)
        YB = scratch("YB", [L, 512], F32)
        X1 = scratch("X1", [L, D], F32)
        MID = scratch("MID", [FC, 128, L], BF16)
        HTD = scratch("HTD", [128, KC, LT], BF16)
        GD = scratch("GD", [128, NT, 104], F32)
        MODD = scratch("MODD", [128, 64 + 2 * D + 24], F32)

        cst = S.sb(es0, "cst", [128, 1024], F32)
        identf = cst.t[:, 0:128]
        trif = cst.t[:, 128:256]
        trib = cst.t[:, 256:384]
        onesf = cst.t[:, 384:512]
        cstb = S.sb(es0, "cstb", [128, 512], BF16)
        identb = cstb.t[:, 0:128]
        onesb = cstb.t[:, 384:512]
        modfm = S.sb(es0, "modfm", [128, 32, 2], F32)
        g1bc = S.sb(es0, "g1bc", [128, D], F32)
        g2bc = S.sb(es0, "g2bc", [128, D], F32)
        AB = S.sb(es0, "AB", [128, 3, KC], F32)
        n1g = S.sb(es0, "n1g", [128, KC], F32)
        n2g = S.sb(es0, "n2g", [128, KC], F32)
        hT = S.sb(es0, "hT", [128, KC, LT], BF16)
        graw = S.sb(es0, "graw", [128, NT, 32], F32)
        g_sp = S.sb(es0, "g_sp", [128, NT, 24], F32)
        g_g = S.sb(es0, "g_g", [128, NT, 24], F32)
        g_gc = S.sb(es0, "g_gc", [128, NT, 24], F32)
        g_tot = S.sb(es0, "g_tot", [128, NT, 24], F32)
        g_beta = S.sb(es0, "g_beta", [128, NT, 8], F32)
        g_egc = S.sb(es0, "g_egc", [128, NT, 24], F32)
        g_ekd = S.sb(es0, "g_ekd", [128, NT, 24], F32)
        g_etot = S.sb(es0, "g_etot", [128, NT, 24], F32)
        g_nbeta = S.sb(es0, "g_nbeta", [128, NT, 8], F32)
        g_bek = S.sb(es0, "g_bek", [128, NT, 8], F32)

        epsb = S.sb(es0, "epsb", [128, 4], F32)
        S.op("dve", lambda e: e.memset(epsb.t[:, 0:1], EPS), writes=[epsb])
        S.op("dve", lambda e: e.memset(epsb.t[:, 1:2], 4.0 * EPS), writes=[epsb])
        S.op("dve", lambda e: e.memset(epsb.t[:, 2:3], 1.0), writes=[epsb])
        S.op("dve", lambda e: e.memset(epsb.t[:, 3:4], 0.0), writes=[epsb])
        S.dma("sp", cst.t[:], cst_in.t[:, :], reads=[cst_in], writes=[cst])
        S.dma("sp", n1g.t[:], n1g_in.t[:, :], reads=[n1g_in], writes=[n1g])
        S.dma("sp", n2g.t[:], n2g_in.t[:, :], reads=[n2g_in], writes=[n2g])
        S.op("dve", lambda e: e.tensor_copy(out=cstb.t[:], in_=cst.t[:, 0:512]), reads=[cst], writes=[cstb])

        pf = [S.ps(es0, "pf%d" % i, [128, 512], F32) for i in range(6)]
        pb = [S.ps(es0, "pb%d" % i, [128, 1024], BF16) for i in range(2)]

        with ExitStack() as es:
            c_sb = S.sb(es, "c_sb", [128, KC, 2], F32)
            sc = S.sb(es, "sc", [128, KC, 2], F32)
            cbc = S.sb(es, "cbc", [128, KC, 128], F32)
            bfm = S.sb(es, "bfm", [128, 32], F32)
            brow = S.sb(es, "brow", [1, 2048], F32)
            wa = [S.sb(es, "wa%d" % i, [128, KC, 512], F32) for i in range(2)]
            S.dma("sp", c_sb.t[:], cT_in.t[:, :, :], reads=[cT_in], writes=[c_sb])
            S.dma("sp", bfm.t[:], bfm_in.t[:, :], reads=[bfm_in], writes=[bfm])
            S.dma("sp", brow.t[:], brow_in.t[:, :], reads=[brow_in], writes=[brow])
            S.op("act", lambda e: e.activation(out=sc.t[:], in_=c_sb.t[:], func=AF.Silu), reads=[c_sb], writes=[sc])
            for kc in range(KC):
                S.op("dve", lambda e, kc=kc: e.tensor_scalar(out=cbc.t[:, kc, :], in0=onesf, scalar1=sc.t[:, kc, 0:1],
                                                           scalar2=None, op0=ALU.mult), reads=[sc, cst], writes=[cbc])
            wada_v = wada_in.t.rearrange("(kc p) n -> p kc n", p=128)
            fm_tiles = [0, 1, 2, 3, 6, 7, 8, 9]
            order = [0, 1, 2, 3, 4, 5, 6, 7, 8, 9, 10, 11]
            for i, n in enumerate(order):
                w = wa[i % 2]
                S.dma("sp", w.t[:], wada_v[:, :, n * 512:(n + 1) * 512], reads=[wada_in], writes=[w])
                if n in fm_tiles:
                    base = fm_tiles.index(n) * 4
                    for f in range(4):
                        for kc in range(KC):
                            S.op("pe", lambda e, w=w, f=f, kc=kc, base=base: e.matmul(
                                pf[0].t[:, (base + f) * 2:(base + f) * 2 + 2], lhsT=w.t[:, kc, f * 128:(f + 1) * 128],
                                rhs=sc.t[:, kc, :], start=(kc == 0), stop=(kc == KC - 1)),
                                reads=[w, sc], writes=[pf[0]], inc=(kc == KC - 1))
                else:
                    j = [4, 5, 10, 11].index(n)
                    pbk = pf[1 + (j % 2)]
                    for kc in range(KC):
                        S.op("pe", lambda e, w=w, kc=kc, pbk=pbk: e.matmul(
                            pbk.t[:, :], lhsT=cbc.t[:, kc, :], rhs=w.t[:, kc, :], start=(kc == 0), stop=False),
                            reads=[w, cbc], writes=[pbk], inc=False)
                    S.op("pe", lambda e, pbk=pbk, j=j: e.matmul(
                        pbk.t[:, :], lhsT=onesf[0:1, :], rhs=brow.t[0:1, j * 512:(j + 1) * 512], start=False, stop=True),
                        reads=[brow, cst], writes=[pbk])
                    dst = g1bc if j < 2 else g2bc
                    S.op("act", lambda e, pbk=pbk, dst=dst, j=j: e.copy(out=dst.t[:, (j % 2) * 512:(j % 2 + 1) * 512], in_=pbk.t[:, :]),
                         reads=[pbk], writes=[dst])
            for j in range(2):
                S.op("dve", lambda e, j=j: e.tensor_tensor(
                    out=modfm.t[:, :, j], in0=pf[0].t[:, 0:64].rearrange("p (c j) -> p c j", j=2)[:, :, j], in1=bfm.t[:, :], op=ALU.add),
                    reads=[pf[0], bfm], writes=[modfm])
            S.op("dve", lambda e: e.scalar_tensor_tensor(out=AB.t[:, 0, :], in0=modfm.t[:, 8:16, 0], scalar=1.0, in1=n1g.t[:, :],
                                                         op0=ALU.add, op1=ALU.mult), reads=[modfm, n1g], writes=[AB])
            S.op("dve", lambda e: e.scalar_tensor_tensor(out=AB.t[:, 1, :], in0=modfm.t[:, 8:16, 1], scalar=1.0, in1=n1g.t[:, :],
                                                         op0=ALU.add, op1=ALU.mult), reads=[modfm, n1g], writes=[AB])
            S.op("dve", lambda e: e.scalar_tensor_tensor(out=AB.t[:, 2, :], in0=modfm.t[:, 24:32, 0], scalar=1.0, in1=n2g.t[:, :],
                                                         op0=ALU.add, op1=ALU.mult), reads=[modfm, n2g], writes=[AB])
            S.barrier()

        def norm_to_T(es, src_ap_fn, ntiles, tile_cfg, dstT, src_buf):
            xt = [S.sb(es, "xt%d" % i, [128, D], F32) for i in range(2)]
            junk = S.sb(es, "junk", [128, D], BF16)
            xn = [S.sb(es, "xn%d" % i, [128, D], BF16) for i in range(2)]
            ss = [S.sb(es, "ss%d" % i, [128, 2], F32) for i in range(2)]
            for t in range(ntiles):
                xb, xnb, ssb, pt = xt[t % 2], xn[t % 2], ss[t % 2], pb[t % 2]
                S.dma("sp", xb.t[:], src_ap_fn(t), reads=[src_buf], writes=[xb])
                S.op("act", lambda e, xb=xb, ssb=ssb: e.activation(out=junk.t[:], in_=xb.t[:], func=AF.Square, accum_out=ssb.t[:, 0:1]),
                     reads=[xb], writes=[junk, ssb])
                S.op("act", lambda e, ssb=ssb: e.activation(out=ssb.t[:, 1:2], in_=ssb.t[:, 0:1], func=AF.Sqrt, scale=1.0 / D, bias=epsb.t[:, 0:1]),
                     reads=[ssb, epsb], writes=[ssb])
                S.op("dve", lambda e, ssb=ssb: e.reciprocal(out=ssb.t[:, 0:1], in_=ssb.t[:, 1:2]), reads=[ssb], writes=[ssb])
                S.op("act", lambda e, xb=xb, xnb=xnb, ssb=ssb: e.activation(out=xnb.t[:], in_=xb.t[:], func=AF.Copy, scale=ssb.t[:, 0:1]),
                     reads=[xb, ssb], writes=[xnb])
                for kc in range(KC):
                    S.op("pe", lambda e, kc=kc, xnb=xnb, pt=pt: e.transpose(out=pt.t[:, kc * 128:(kc + 1) * 128],
                                                                            in_=xnb.t[:, kc * 128:(kc + 1) * 128], identity=identb),
                         reads=[xnb, cstb], writes=[pt], inc=(kc == KC - 1))
                ai, bap, tok0 = tile_cfg(t)
                for kc in range(KC):
                    S.op("dve", lambda e, kc=kc, pt=pt, ai=ai, bap=bap, tok0=tok0: e.tensor_scalar(
                        out=dstT.t[:, kc, tok0:tok0 + 128], in0=pt.t[:, kc * 128:(kc + 1) * 128],
                        scalar1=AB.t[:, ai, kc:kc + 1], scalar2=bap(kc), op0=ALU.mult, op1=ALU.add),
                        reads=[pt, AB, modfm], writes=[dstT])

        with ExitStack() as es:
            def src1(t):
                return ctx_in.t[t * 128:(t + 1) * 128, :] if t < NTC else x_in.t[(t - NTC) * 128:(t - NTC + 1) * 128, :]

            def cfg1(t):
                if t < NTC:
                    return 1, (lambda kc: modfm.t[:, kc, 1:2]), t * 128
                return 0, (lambda kc: modfm.t[:, kc, 0:1]), t * 128
            norm_to_T(es, src1, NT, cfg1, hT, x_in)
            S.barrier()
        if debug and "HTD" in debug:
            S.dma("sp", HTD.t[:, :, :], hT.t[:], reads=[hT], writes=[HTD])
        if debug and "MODD" in debug:
            S.dma("sp", MODD.t[:, 0:64], modfm.t[:].rearrange("p c j -> p (c j)"), reads=[modfm], writes=[MODD])
            S.dma("sp", MODD.t[:, 64:64 + D], g1bc.t[:], reads=[g1bc], writes=[MODD])
            S.dma("sp", MODD.t[:, 64 + D:64 + 2 * D], g2bc.t[:], reads=[g2bc], writes=[MODD])
            S.dma("sp", MODD.t[:, 64 + 2 * D:64 + 2 * D + 24], AB.t[:].rearrange("p a k -> p (a k)"), reads=[AB], writes=[MODD])


        win_v = win_in.t.rearrange("(kc p) n -> p kc n", p=128)
        with ExitStack() as es:
            cw = S.sb(es, "cw", [128, 20, 3], F32)
            cbias = S.sb(es, "cbias", [128, 20], F32)
            S.dma("sp", cw.t[:], cw_in.t[:, :, :], reads=[cw_in], writes=[cw])
            S.dma("sp", cbias.t[:], cb_in.t[:, :], reads=[cb_in], writes=[cbias])
            wg = [S.sb(es, "wg%d" % i, [128, KC, 512], BF16) for i in range(2)]
            rowbuf = [S.sb(es, "rowbuf%d" % i, [128, LT + 4], BF16) for i in range(2)]
            dg = [S.sb(es, "dg%d" % i, [128, 3, 128], BF16) for i in range(2)]
            arow = S.sb(es, "arow", [128, LT], F32)
            sqb = [S.sb(es, "sqb%d" % i, [128, 512], BF16) for i in range(2)]
            srt = [S.sb(es, "srt%d" % i, [128, 512], F32) for i in range(2)]
            ab16 = [S.sb(es, "ab16_%d" % i, [128, 512], BF16) for i in range(3)]
            tmst = [S.sb(es, "tmst%d" % i, [128, 4, 128], BF16) for i in range(2)]
            for rb in rowbuf:
                S.op("pool", lambda e, rb=rb: e.memset(rb.t[:], 0.0), writes=[rb])
            blocks = [(1, 0, LC)] + [(259 + i * 512, LC + i * 512, 512) for i in range(L // 512)]
            groups = [("q", C_Q), ("k", C_K), ("v", C_V), ("xs", C_XS), ("bc", C_BS)]
            cnt = {"ps": 0, "ab": 0, "tm": 0, "ev": 0}
            chunk_idx = 0
            for gi, (gname, c0) in enumerate(groups):
                wgb = wg[gi % 2]
                S.dma("pool", wgb.t[:], win_v[:, :, c0:c0 + 512], reads=[win_in], writes=[wgb])
                for ci in range(4):
                    cidx = {"q": 0, "k": 4, "v": 8, "xs": 12, "bc": 16}[gname] + ci
                    rb = rowbuf[chunk_idx % 2]
                    dgb = dg[chunk_idx % 2]
                    chunk_idx += 1
                    for tap in range(3):
                        S.op("dve", lambda e, dgb=dgb, tap=tap, cidx=cidx: e.tensor_scalar(
                            out=dgb.t[:, tap, :], in0=identb, scalar1=cw.t[:, cidx, tap:tap + 1], scalar2=None, op0=ALU.mult),
                            reads=[cstb, cw], writes=[dgb])
                    for (ro, t0, ntk) in blocks:
                        p = pf[cnt["ps"] % 3]
                        cnt["ps"] += 1
                        for kc in range(KC):
                            S.op("pe", lambda e, p=p, kc=kc, t0=t0, ntk=ntk, wgb=wgb, ci=ci: e.matmul(
                                p.t[:, 0:ntk], lhsT=wgb.t[:, kc, ci * 128:(ci + 1) * 128], rhs=hT.t[:, kc, t0:t0 + ntk],
                                start=(kc == 0), stop=(kc == KC - 1)), reads=[wgb, hT], writes=[p], inc=(kc == KC - 1))
                        eng = "act" if cnt["ev"] % 2 == 0 else "dve"
                        cnt["ev"] += 1
                        if eng == "act":
                            S.op("act", lambda e, p=p, rb=rb, ro=ro, ntk=ntk: e.copy(out=rb.t[:, ro:ro + ntk], in_=p.t[:, 0:ntk]),
                                 reads=[p], writes=[rb])
                        else:
                            S.op("dve", lambda e, p=p, rb=rb, ro=ro, ntk=ntk: e.tensor_copy(out=rb.t[:, ro:ro + ntk], in_=p.t[:, 0:ntk]),
                                 reads=[p], writes=[rb])
                    is_qk = gname in ("q", "k")
                    for (ro, t0, ntk) in blocks:
                        p = pf[cnt["ps"] % 3]
                        cnt["ps"] += 1
                        for tap in range(3):
                            S.op("pe", lambda e, p=p, tap=tap, ro=ro, ntk=ntk, dgb=dgb, rb=rb: e.matmul(
                                p.t[:, 0:ntk], lhsT=dgb.t[:, tap, :], rhs=rb.t[:, ro + tap - 1:ro + tap - 1 + ntk],
                                start=(tap == 0), stop=(tap == 2)), reads=[dgb, rb], writes=[p], inc=(tap == 2))
                        if is_qk:
                            S.op("act", lambda e, p=p, t0=t0, ntk=ntk: e.activation(out=arow.t[:, t0:t0 + ntk], in_=p.t[:, 0:ntk], func=AF.Silu),
                                 reads=[p], writes=[arow])
                        else:
                            a16 = ab16[cnt["ab"] % 3]
                            cnt["ab"] += 1
                            S.op("act", lambda e, p=p, ntk=ntk, a16=a16, cidx=cidx: e.activation(
                                out=a16.t[:, 0:ntk], in_=p.t[:, 0:ntk], func=AF.Silu, bias=cbias.t[:, cidx:cidx + 1]),
                                reads=[p, cbias], writes=[a16])
                            post_block(S, gname, ci, a16, t0, ntk, cnt, tmst, pb, identb, cstb,
                                       dict(QT=QT, KT=KT, Ktm=Ktm, Vtm=Vtm, Xtm=Xtm, BT=BT, CT=CT, Btm=Btm))
                    if is_qk:
                        for bi, (ro, t0, ntk) in enumerate(blocks):
                            sq = sqb[bi % 2]
                            S.op("act", lambda e, sq=sq, t0=t0, ntk=ntk: e.activation(out=sq.t[:, 0:ntk], in_=arow.t[:, t0:t0 + ntk], func=AF.Square),
                                 reads=[arow], writes=[sq])
                            p = pf[3 + bi % 2]
                            S.op("pe", lambda e, p=p, sq=sq, ntk=ntk: e.matmul(p.t[:, 0:ntk], lhsT=onesb, rhs=sq.t[:, 0:ntk], start=True, stop=True),
                                 reads=[cstb, sq], writes=[p])
                            sr = srt[bi % 2]
                            S.op("act", lambda e, p=p, sr=sr, ntk=ntk: e.activation(out=sr.t[:, 0:ntk], in_=p.t[:, 0:ntk], func=AF.Sqrt, bias=epsb.t[:, 0:1]),
                                 reads=[p, epsb], writes=[sr])
                            S.op("dve", lambda e, sr=sr, ntk=ntk: e.reciprocal(out=sr.t[:, 0:ntk], in_=sr.t[:, 0:ntk]), reads=[sr], writes=[sr])
                            a16 = ab16[cnt["ab"] % 3]
                            cnt["ab"] += 1
                            scale = (128.0 ** -0.5) if gname == "q" else 1.0
                            S.op("dve", lambda e, a16=a16, sr=sr, t0=t0, ntk=ntk, scale=scale: e.scalar_tensor_tensor(
                                out=a16.t[:, 0:ntk], in0=arow.t[:, t0:t0 + ntk], scalar=scale, in1=sr.t[:, 0:ntk], op0=ALU.mult, op1=ALU.mult),
                                reads=[arow, sr], writes=[a16])
                            post_block(S, gname, ci, a16, t0, ntk, cnt, tmst, pb, identb, cstb,
                                       dict(QT=QT, KT=KT, Ktm=Ktm, Vtm=Vtm, Xtm=Xtm, BT=BT, CT=CT, Btm=Btm))
            S.barrier()

        with ExitStack() as es:
            wtm = S.sb(es, "wtm", [128, KC, 1056], BF16)
            gcb = S.sb(es, "gcb", [128, NT, 24], F32)
            gca = S.sb(es, "gca", [128, NT, 24], F32)
            zst = [S.sb(es, "zst%d" % i, [128, 512], BF16) for i in range(3)]
            S.dma("pool", wtm.t[:, :, 0:512], win_v[:, :, C_ZG:C_ZG + 512], reads=[win_in], writes=[wtm])
            S.dma("pool", wtm.t[:, :, 512:1024], win_v[:, :, C_ZS:C_ZS + 512], reads=[win_in], writes=[wtm])
            S.dma("pool", wtm.t[:, :, 1024:1032], win_v[:, :, C_A:C_A + 8], reads=[win_in], writes=[wtm])
            S.dma("pool", wtm.t[:, :, 1032:1048], win_v[:, :, C_DT:C_DT + 16], reads=[win_in], writes=[wtm])
            S.dma("pool", wtm.t[:, :, 1048:1056], win_v[:, :, C_B:C_B + 8], reads=[win_in], writes=[wtm])
            S.dma("sp", gcb.t[:], gcb_in.t[:, :, :], reads=[gcb_in], writes=[gcb])
            S.dma("sp", gca.t[:], gca_in.t[:, :, :], reads=[gca_in], writes=[gca])
            zc = 0
            for t in range(NT):
                tk = slice(t * 128, (t + 1) * 128)
                p = pf[t % 2]
                for kc in range(KC):
                    S.op("pe", lambda e, p=p, kc=kc, tk=tk: e.matmul(p.t[:, 0:32], lhsT=hT.t[:, kc, tk], rhs=wtm.t[:, kc, 1024:1056],
                                                                     start=(kc == 0), stop=(kc == KC - 1)),
                         reads=[hT, wtm], writes=[p], inc=(kc == KC - 1))
                S.op("dve", lambda e, p=p, t=t: e.tensor_copy(out=graw.t[:, t, :], in_=p.t[:, 0:32]), reads=[p], writes=[graw])
                if t >= NTC:
                    for zi, ZD in enumerate((ZG, ZS)):
                        p2 = pf[2 + zc % 3]
                        zb = zst[zc % 3]
                        zc += 1
                        for kc in range(KC):
                            S.op("pe", lambda e, p2=p2, kc=kc, tk=tk, zi=zi: e.matmul(
                                p2.t[:, :], lhsT=hT.t[:, kc, tk], rhs=wtm.t[:, kc, zi * 512:(zi + 1) * 512],
                                start=(kc == 0), stop=(kc == KC - 1)), reads=[hT, wtm], writes=[p2], inc=(kc == KC - 1))
                        S.op("act", lambda e, p2=p2, zb=zb: e.activation(out=zb.t[:], in_=p2.t[:, :], func=AF.Silu), reads=[p2], writes=[zb])
                        S.dma("sp", ZD.t[(t - NTC) * 128:(t - NTC + 1) * 128, :], zb.t[:], reads=[zb], writes=[ZD])
            tmpg = S.sb(es, "tmpg", [128, NT, 24], F32)
            S.op("dve", lambda e: e.tensor_tensor(out=tmpg.t[:], in0=graw.t[:, :, 0:24], in1=gcb.t[:], op=ALU.add), reads=[graw, gcb], writes=[tmpg])
            S.op("act", lambda e: e.activation(out=tmpg.t[:], in_=tmpg.t[:], func=AF.Exp), reads=[tmpg], writes=[tmpg])
            S.op("act", lambda e: e.activation(out=g_sp.t[:], in_=tmpg.t[:], func=AF.Ln, bias=epsb.t[:, 2:3]), reads=[tmpg, epsb], writes=[g_sp])
            S.op("act", lambda e: e.activation(out=gca.t[:], in_=gca.t[:], func=AF.Exp), reads=[gca], writes=[gca])
            S.op("dve", lambda e: e.scalar_tensor_tensor(out=g_g.t[:], in0=g_sp.t[:], scalar=-1.0, in1=gca.t[:], op0=ALU.mult, op1=ALU.mult),
                 reads=[g_sp, gca], writes=[g_g])
            tmpb = S.sb(es, "tmpb", [128, NT, 8], F32)
            S.op("act", lambda e: e.activation(out=tmpb.t[:], in_=graw.t[:, :, 24:32], func=AF.Exp, scale=-1.0), reads=[graw], writes=[tmpb])
            S.op("dve", lambda e: e.tensor_scalar(out=tmpb.t[:], in0=tmpb.t[:], scalar1=1.0, scalar2=None, op0=ALU.add), reads=[tmpb], writes=[tmpb])
            S.op("dve", lambda e: e.reciprocal(out=g_beta.t[:], in_=tmpb.t[:]), reads=[tmpb], writes=[g_beta])
            gview = g_g.t
            pA, pB, pC = pf[0], pf[1], pf[2]
            pAv = pA.t[:, 0:NT * 8].rearrange("p (t c) -> p t c", c=8)
            pBv = pB.t[:, 0:NT * 8].rearrange("p (t c) -> p t c", c=8)
            pCv = pC.t[:, 0:NT * 8].rearrange("p (t c) -> p t c", c=8)
            S.op("pe", lambda e: e.matmul(pAv[:, :, 0:4], lhsT=trif, rhs=gview[:, :, 0:4], start=True, stop=True), reads=[cst, g_g], writes=[pA])
            S.op("pe", lambda e: e.matmul(pAv[:, :, 4:8], lhsT=trib, rhs=gview[:, :, 4:8], start=True, stop=True), reads=[cst, g_g], writes=[pA])
            S.op("pe", lambda e: e.matmul(pBv[:, :, :], lhsT=trif, rhs=gview[:, :, 8:16], start=True, stop=True), reads=[cst, g_g], writes=[pB])
            S.op("pe", lambda e: e.matmul(pCv[:, :, :], lhsT=trib, rhs=gview[:, :, 16:24], start=True, stop=True), reads=[cst, g_g], writes=[pC])
            S.op("dve", lambda e: e.tensor_copy(out=g_gc.t[:, :, 0:8], in_=pAv), reads=[pA], writes=[g_gc])
            S.op("dve", lambda e: e.tensor_copy(out=g_gc.t[:, :, 8:16], in_=pBv), reads=[pB], writes=[g_gc])
            S.op("dve", lambda e: e.tensor_copy(out=g_gc.t[:, :, 16:24], in_=pCv), reads=[pC], writes=[g_gc])
            pD, pE = pf[3], pf[4]
            gflat = g_g.t[:].rearrange("p t c -> p (t c)")
            S.op("pe", lambda e: e.matmul(pD.t[:, 0:408], lhsT=onesf, rhs=gflat[:, 0:408], start=True, stop=True), reads=[cst, g_g], writes=[pD])
            S.op("pe", lambda e: e.matmul(pE.t[:, 0:408], lhsT=onesf, rhs=gflat[:, 408:816], start=True, stop=True), reads=[cst, g_g], writes=[pE])
            tflat = g_tot.t[:].rearrange("p t c -> p (t c)")
            S.op("dve", lambda e: e.tensor_copy(out=tflat[:, 0:408], in_=pD.t[:, 0:408]), reads=[pD], writes=[g_tot])
            S.op("dve", lambda e: e.tensor_copy(out=tflat[:, 408:816], in_=pE.t[:, 0:408]), reads=[pE], writes=[g_tot])
            S.op("act", lambda e: e.activation(out=g_egc.t[:], in_=g_gc.t[:], func=AF.Exp), reads=[g_gc], writes=[g_egc])
            S.op("act", lambda e: e.activation(out=g_etot.t[:], in_=g_tot.t[:], func=AF.Exp), reads=[g_tot], writes=[g_etot])
            S.op("dve", lambda e: e.tensor_tensor(out=g_ekd.t[:], in0=g_tot.t[:], in1=g_gc.t[:], op=ALU.subtract), reads=[g_tot, g_gc], writes=[g_ekd])
            S.op("act", lambda e: e.activation(out=g_ekd.t[:], in_=g_ekd.t[:], func=AF.Exp), reads=[g_ekd], writes=[g_ekd])
            S.op("dve", lambda e: e.tensor_scalar(out=g_nbeta.t[:], in0=g_beta.t[:], scalar1=-1.0, scalar2=None, op0=ALU.mult), reads=[g_beta], writes=[g_nbeta])
            S.op("dve", lambda e: e.tensor_tensor(out=g_bek.t[:], in0=g_beta.t[:], in1=g_egc.t[:, :, 0:8], op=ALU.mult), reads=[g_beta, g_egc], writes=[g_bek])
            gcflat = g_gc.t[:].rearrange("p t c -> p (t c)")
            gct_sb = S.sb(es, "gct_sb", [128, 7, 128], F32)
            for blk in range(7):
                nr = min(128, NT * 24 - blk * 128)
                p = pf[blk % 2]
                S.op("pe", lambda e, p=p, blk=blk, nr=nr: e.transpose(out=p.t[0:nr, 0:128], in_=gcflat[:, blk * 128:blk * 128 + nr], identity=identf),
                     reads=[g_gc, cst], writes=[p])
                S.op("dve", lambda e, p=p, blk=blk, nr=nr: e.tensor_copy(out=gct_sb.t[0:nr, blk, :], in_=p.t[0:nr, 0:128]), reads=[p], writes=[gct_sb])
                S.dma("sp", GCT.t[blk * 128:blk * 128 + nr, :], gct_sb.t[0:nr, blk, :], reads=[gct_sb], writes=[GCT])
            if debug and "GD" in debug:
                for i, gb in enumerate((g_sp, g_g, g_gc, g_tot)):
                    S.dma("sp", GD.t[:, :, i * 24:(i + 1) * 24], gb.t[:], reads=[gb], writes=[GD])
                S.dma("sp", GD.t[:, :, 96:104], g_beta.t[:], reads=[g_beta], writes=[GD])
            S.barrier()


        def bc3(ap2, n):
            return ap2.unsqueeze(2).to_broadcast([128, ap2.shape[1], n])
        pbf = [Buf("pbf%d" % i, None) for i in range(2)]
        pbv = [pb[i].t[:].bitcast(F32) for i in range(2)]
        if GDN_ON:
          with ExitStack() as es:
            msk = S.sb(es, "msk", [128, 16, 128], F32)
            S.dma("sp", msk.t[:], msk_in.t[:, 0:16, :], reads=[msk_in], writes=[msk])
            kTt = [S.sb(es, "kTt%d" % i, [128, 4, 128], BF16) for i in range(2)]
            qTt = [S.sb(es, "qTt%d" % i, [128, 4, 128], BF16) for i in range(2)]
            ktmt = [S.sb(es, "ktmt%d" % i, [128, 4, 128], BF16) for i in range(2)]
            vtmt = [S.sb(es, "vtmt%d" % i, [128, 4, 128], BF16) for i in range(2)]
            rowb = [S.sb(es, "rowb%d" % i, [128, 8, 128], F32) for i in range(2)]
            eraw = S.sb(es, "eraw", [128, 8, 128], F32)
            m1 = S.sb(es, "m1", [128, 8, 128], F32)
            m2 = S.sb(es, "m2", [128, 8, 128], F32)
            er = S.sb(es, "er", [128, 8, 128], F32)
            tmpw = [S.sb(es, "tmpw%d" % i, [128, 4, 128], F32) for i in range(2)]
            Wb = [[S.sb(es, "W%d_%d" % (d, i), [128, 4, 128], F32) for i in range(2)] for d in range(2)]
            Zb = [[S.sb(es, "Z%d_%d" % (d, i), [128, 4, 128], F32) for i in range(2)] for d in range(2)]
            Pb = [[S.sb(es, "P%d_%d" % (d, i), [128, 4, 128], F32) for i in range(2)] for d in range(2)]
            PTb = [S.sb(es, "PTb%d" % d, [128, 4, 128], BF16) for d in range(2)]
            kbe = [S.sb(es, "kbe%d" % d, [128, 4, 128], BF16) for d in range(2)]
            vbb = [S.sb(es, "vbb%d" % d, [128, 4, 128], BF16) for d in range(2)]
            stq = [[S.sb(es, "stq%d_%d" % (d, i), [128, 4, 512], BF16) for i in range(2)] for d in range(2)]
            bankA = [pf[0], pf[1]]
            bankB = [pf[2], pf[3]]
            bankC = [pf[4], pf[5]]

            def v4(b):
                return b.t[:, :].rearrange("p (h k) -> p h k", h=4)
            for t in range(NT):
                tk = slice(t * 128, (t + 1) * 128)
                kT, qT, ktm, vtm, rb = kTt[t % 2], qTt[t % 2], ktmt[t % 2], vtmt[t % 2], rowb[t % 2]
                S.dma("sp", kT.t[:], KT.t[:, :, tk].rearrange("h p k -> p h k"), reads=[KT], writes=[kT])
                S.dma("sp", qT.t[:], QT.t[:, :, tk].rearrange("h p k -> p h k"), reads=[QT], writes=[qT])
                S.dma("sp", ktm.t[:].rearrange("p h k -> p (h k)"), Ktm.t[tk, :], reads=[Ktm], writes=[ktm])
                S.dma("sp", vtm.t[:].rearrange("p h k -> p (h k)"), Vtm.t[tk, :], reads=[Vtm], writes=[vtm])
                S.dma("sp", rb.t[:], GCT.t[t * 24:t * 24 + 8, :].partition_broadcast(128), reads=[GCT], writes=[rb])
                for h in range(4):
                    S.op("pe", lambda e, h=h, kT=kT: e.matmul(pbv[0][:, h * 128:(h + 1) * 128], lhsT=kT.t[:, h, :], rhs=kT.t[:, h, :], start=True, stop=True),
                         reads=[kT], writes=[pb[0]], inc=(h == 3))
                for h in range(4):
                    S.op("pe", lambda e, h=h, kT=kT, qT=qT: e.matmul(pbv[1][:, h * 128:(h + 1) * 128], lhsT=kT.t[:, h, :], rhs=qT.t[:, h, :], start=True, stop=True),
                         reads=[kT, qT], writes=[pb[1]], inc=(h == 3))
                S.op("dve", lambda e, rb=rb, t=t: e.tensor_tensor(out=eraw.t[:], in0=rb.t[:], in1=bc3(g_gc.t[:, t, 0:8], 128), op=ALU.subtract),
                     reads=[rb, g_gc], writes=[eraw])
                S.op("dve", lambda e: e.tensor_tensor(out=m1.t[:], in0=eraw.t[:], in1=msk.t[:, 0:8, :], op=ALU.add), reads=[eraw, msk], writes=[m1])
                S.op("pool", lambda e: e.tensor_tensor(out=m2.t[:], in0=msk.t[:, 8:16, :], in1=eraw.t[:], op=ALU.subtract), reads=[eraw, msk], writes=[m2])
                S.op("act", lambda e: e.activation(out=m1.t[:], in_=m1.t[:], func=AF.Exp), reads=[m1], writes=[m1])
                S.op("act", lambda e: e.activation(out=m2.t[:], in_=m2.t[:], func=AF.Exp), reads=[m2], writes=[m2])
                S.op("act", lambda e, rb=rb: e.activation(out=er.t[:], in_=rb.t[:], func=AF.Exp), reads=[rb], writes=[er])
                sq_ = [stq[d][t % 2] for d in range(2)]
                for d in range(2):
                    cs = slice(d * 4, d * 4 + 4)
                    S.op("dve", lambda e, d=d, cs=cs: e.tensor_tensor(out=tmpw[d].t[:], in0=pbv[0].rearrange("p (h k) -> p h k", h=4), in1=m2.t[:, cs, :], op=ALU.mult),
                         reads=[pb[0], m2], writes=[tmpw[d]])
                    S.op("dve", lambda e, d=d, cs=cs, t=t: e.tensor_tensor(out=Wb[d][0].t[:], in0=tmpw[d].t[:], in1=bc3(g_nbeta.t[:, t, cs], 128), op=ALU.mult),
                         reads=[tmpw[d], g_nbeta], writes=[Wb[d][0]])
                    if t >= NTC:
                        S.op("dve", lambda e, d=d, cs=cs, sq_=sq_: e.tensor_tensor(out=sq_[d].t[:, 2, :].rearrange("p (h k) -> p h k", h=4),
                                                                        in0=pbv[1].rearrange("p (h k) -> p h k", h=4), in1=m1.t[:, cs, :], op=ALU.mult),
                             reads=[pb[1], m1], writes=[sq_[d]])
                        S.op("pool", lambda e, d=d, cs=cs, qT=qT, sq_=sq_: e.tensor_tensor(out=sq_[d].t[:, 3, :].rearrange("p (h k) -> p h k", h=4),
                                                                                 in0=qT.t[:], in1=er.t[:, cs, :], op=ALU.mult),
                             reads=[qT, er], writes=[sq_[d]])
                    S.op("pool", lambda e, d=d, cs=cs, ktm=ktm, t=t: e.tensor_tensor(out=kbe[d].t[:], in0=ktm.t[:], in1=bc3(g_bek.t[:, t, cs], 128), op=ALU.mult),
                         reads=[ktm, g_bek], writes=[kbe[d]])
                    S.op("pool", lambda e, d=d, cs=cs, vtm=vtm, t=t: e.tensor_tensor(out=vbb[d].t[:], in0=vtm.t[:], in1=bc3(g_beta.t[:, t, cs], 128), op=ALU.mult),
                         reads=[vtm, g_beta], writes=[vbb[d]])
                for d in range(2):
                    for h in range(4):
                        S.op("pe", lambda e, d=d, h=h: e.transpose(out=bankB[d].t[:, h * 128:(h + 1) * 128], in_=Wb[d][0].t[:, h, :], identity=identf),
                             reads=[Wb[d][0], cst], writes=[bankB[d]], inc=(h == 3))
                for d in range(2):
                    S.op("act", lambda e, d=d: e.copy(out=Zb[d][0].t[:], in_=v4(bankB[d])), reads=[bankB[d]], writes=[Zb[d][0]])
                    S.op("dve", lambda e, d=d: e.tensor_tensor(out=Pb[d][0].t[:], in0=Zb[d][0].t[:], in1=identf.unsqueeze(1).to_broadcast([128, 4, 128]), op=ALU.add),
                         reads=[Zb[d][0], cst], writes=[Pb[d][0]])
                NSTEP = 6
                for m in range(NSTEP):
                    cur, nxt = m % 2, 1 - m % 2
                    last = (m == NSTEP - 1)
                    for d in range(2):
                        for h in range(4):
                            S.op("pe", lambda e, d=d, h=h, cur=cur: e.matmul(bankA[d].t[:, h * 128:(h + 1) * 128], lhsT=Zb[d][cur].t[:, h, :], rhs=Wb[d][cur].t[:, h, :],
                                                                            start=True, stop=True), reads=[Zb[d][cur], Wb[d][cur]], writes=[bankA[d]], inc=(h == 3))
                    if not last:
                        for d in range(2):
                            for h in range(4):
                                S.op("pe", lambda e, d=d, h=h, cur=cur: e.matmul(bankB[d].t[:, h * 128:(h + 1) * 128], lhsT=Wb[d][cur].t[:, h, :], rhs=Zb[d][cur].t[:, h, :],
                                                                                start=True, stop=True), reads=[Zb[d][cur], Wb[d][cur]], writes=[bankB[d]], inc=(h == 3))
                    for d in range(2):
                        S.op("act", lambda e, d=d, nxt=nxt: e.copy(out=Wb[d][nxt].t[:], in_=v4(bankA[d])), reads=[bankA[d]], writes=[Wb[d][nxt]])
                    if not last:
                        for d in range(2):
                            if d == 0:
                                S.op("dve", lambda e, d=d, nxt=nxt: e.tensor_copy(out=Zb[d][nxt].t[:], in_=v4(bankB[d])), reads=[bankB[d]], writes=[Zb[d][nxt]])
                            else:
                                S.op("act", lambda e, d=d, nxt=nxt: e.copy(out=Zb[d][nxt].t[:], in_=v4(bankB[d])), reads=[bankB[d]], writes=[Zb[d][nxt]])
                    for d in range(2):
                        for h in range(4):
                            S.op("pe", lambda e, d=d, h=h, cur=cur, nxt=nxt: e.matmul(bankC[d].t[:, h * 128:(h + 1) * 128], lhsT=Wb[d][nxt].t[:, h, :], rhs=Pb[d][cur].t[:, h, :],
                                                                                     start=True, stop=True), reads=[Wb[d][nxt], Pb[d][cur]], writes=[bankC[d]], inc=(h == 3))
                    for d in range(2):
                        dst = PTb[d] if last else Pb[d][nxt]
                        S.op("dve", lambda e, d=d, cur=cur, dst=dst: e.tensor_tensor(out=dst.t[:], in0=v4(bankC[d]), in1=Pb[d][cur].t[:], op=ALU.add),
                             reads=[bankC[d], Pb[d][cur]], writes=[dst])
                for d in range(2):
                    for h in range(4):
                        S.op("pe", lambda e, d=d, h=h: e.matmul(bankA[d].t[:, h * 128:(h + 1) * 128], lhsT=PTb[d].t[:, h, :], rhs=vbb[d].t[:, h, :], start=True, stop=True),
                             reads=[PTb[d], vbb[d]], writes=[bankA[d]], inc=(h == 3))
                    for h in range(4):
                        S.op("pe", lambda e, d=d, h=h: e.matmul(bankB[d].t[:, h * 128:(h + 1) * 128], lhsT=kbe[d].t[:, h, :], rhs=PTb[d].t[:, h, :], start=True, stop=True),
                             reads=[PTb[d], kbe[d]], writes=[bankB[d]], inc=(h == 3))
                for d in range(2):
                    S.op("act", lambda e, d=d, sq_=sq_: e.copy(out=sq_[d].t[:, 0, :], in_=bankA[d].t[:, :]), reads=[bankA[d]], writes=[sq_[d]])
                    S.op("dve", lambda e, d=d, sq_=sq_: e.tensor_copy(out=sq_[d].t[:, 1, :], in_=bankB[d].t[:, :]), reads=[bankB[d]], writes=[sq_[d]])
                    nk = 4 if t >= NTC else 2
                    S.dma("sp", UWQ.t[t, d, 0:nk].rearrange("k p c -> p k c"), sq_[d].t[:, 0:nk, :], reads=[sq_[d]], writes=[UWQ])
            S.barrier()

          with ExitStack() as es:
            Sf = [[S.sb(es, "Sf%d_%d" % (d, h), [128, 128], F32) for h in range(4)] for d in range(2)]
            Sh = [[S.sb(es, "Sh%d_%d" % (d, h), [128, 128], BF16) for h in range(4)] for d in range(2)]
            uw = [[S.sb(es, "uw%d_%d" % (d, i), [128, 4, 512], BF16) for i in range(3)] for d in range(2)]
            ktm2 = [[S.sb(es, "ktm2_%d_%d" % (d, i), [128, 4, 128], BF16) for i in range(3)] for d in range(2)]
            kdec = [[S.sb(es, "kdec%d_%d" % (d, i), [128, 4, 128], BF16) for i in range(2)] for d in range(2)]
            vnew = [[[S.sb(es, "vn%d_%d_%d" % (d, h, i), [128, 128], BF16) for i in range(2)] for h in range(4)] for d in range(2)]
            ost = [[S.sb(es, "ost%d_%d" % (d, i), [128, 512], F32) for i in range(2)] for d in range(2)]
            for d in range(2):
                for h in range(4):
                    S.op("pool", lambda e, d=d, h=h: e.memset(Sf[d][h].t[:], 0.0), writes=[Sf[d][h]])
                    S.op("pool", lambda e, d=d, h=h: e.memset(Sh[d][h].t[:], 0.0), writes=[Sh[d][h]])
            order_f = list(range(NT))
            order_b = [1, 0] + list(range(NT - 1, NTC - 1, -1))
            orders = [order_f, order_b]
            for step in range(NT if RUN3B else 0):
                for d in range(2):
                    t = orders[d][step]
                    tk = slice(t * 128, (t + 1) * 128)
                    isx = t >= NTC
                    nk = 4 if isx else 2
                    u = uw[d][step % 3]
                    k2 = ktm2[d][step % 3]
                    kd = kdec[d][step % 2]
                    os_ = ost[d][step % 2]
                    cs = slice(d * 4, d * 4 + 4)
                    S.dma("sp", u.t[:, 0:nk, :], UWQ.t[t, d, 0:nk].rearrange("k p c -> p k c"), reads=[UWQ], writes=[u])
                    S.dma("sp", k2.t[:].rearrange("p h k -> p (h k)"), Ktm.t[tk, :], reads=[Ktm], writes=[k2])
                    S.op("pool", lambda e, kd=kd, k2=k2, t=t, cs=cs: e.tensor_tensor(out=kd.t[:], in0=k2.t[:], in1=bc3(g_ekd.t[:, t, cs], 128), op=ALU.mult),
                         reads=[k2, g_ekd], writes=[kd])
                    for h in range(4):
                        c = d * 4 + h
                        hs = slice(h * 128, (h + 1) * 128)
                        pW = pf[2 * d + (h % 2)]
                        vn = vnew[d][h][step % 2]
                        S.op("pe", lambda e, pW=pW, u=u, hs=hs, d=d, h=h: e.matmul(pW.t[:, 0:128], lhsT=u.t[:, 1, hs], rhs=Sh[d][h].t[:], start=True, stop=True),
                             reads=[u, Sh[d][h]], writes=[pW])
                        if isx:
                            S.op("pe", lambda e, pW=pW, u=u, hs=hs, d=d, h=h: e.matmul(pW.t[:, 128:256], lhsT=u.t[:, 3, hs], rhs=Sh[d][h].t[:], start=True, stop=False),
                                 reads=[u, Sh[d][h]], writes=[pW])
                        S.op("dve", lambda e, pW=pW, u=u, hs=hs, vn=vn: e.tensor_tensor(out=vn.t[:], in0=u.t[:, 0, hs], in1=pW.t[:, 0:128], op=ALU.subtract),
                             reads=[u, pW], writes=[vn])
                        if isx:
                            S.op("pe", lambda e, pW=pW, u=u, hs=hs, vn=vn: e.matmul(pW.t[:, 128:256], lhsT=u.t[:, 2, hs], rhs=vn.t[:], start=False, stop=True),
                                 reads=[u, vn], writes=[pW])
                            S.op("act", lambda e, pW=pW, os_=os_, hs=hs: e.copy(out=os_.t[:, hs], in_=pW.t[:, 128:256]), reads=[pW], writes=[os_])
                        pS = pf[4 + d] if h % 2 == 0 else pb[d]
                        pSv = pS.t[:, 0:128] if h % 2 == 0 else pbv[d][:, 0:128]
                        S.op("pe", lambda e, pSv=pSv, kd=kd, h=h, vn=vn: e.matmul(pSv, lhsT=kd.t[:, h, :], rhs=vn.t[:], start=True, stop=True),
                             reads=[kd, vn], writes=[pS])
                        S.op("dve", lambda e, pSv=pSv, d=d, h=h, t=t, c=c: e.scalar_tensor_tensor(out=Sf[d][h].t[:], in0=Sf[d][h].t[:], scalar=g_etot.t[:, t, c:c + 1],
                                                                                                 in1=pSv, op0=ALU.mult, op1=ALU.add),
                             reads=[pS, Sf[d][h], g_etot], writes=[Sf[d][h]])
                        S.op("act", lambda e, d=d, h=h: e.copy(out=Sh[d][h].t[:], in_=Sf[d][h].t[:]), reads=[Sf[d][h]], writes=[Sh[d][h]])
                    if isx:
                        OD = OF if d == 0 else OB
                        S.dma("sp", OD.t[(t - NTC) * 128:(t - NTC + 1) * 128, :], os_.t[:], reads=[os_], writes=[OD])
            S.barrier()


        if SSD_ON:
          with ExitStack() as es:
            msk16 = S.sb(es, "msk16", [128, 16, 128], F32)
            S.dma("sp", msk16.t[:], msk_in.t[:, 16:32, :], reads=[msk_in], writes=[msk16])
            Hf = [S.sb(es, "Hf%d" % d, [128, 8, 64], F32) for d in range(2)]
            Hh = [S.sb(es, "Hh%d" % d, [128, 512], BF16) for d in range(2)]
            for d in range(2):
                S.op("pool", lambda e, d=d: e.memset(Hf[d].t[:], 0.0), writes=[Hf[d]])
                S.op("pool", lambda e, d=d: e.memset(Hh[d].t[:], 0.0), writes=[Hh[d]])
            BTt = [[S.sb(es, "BTt%d_%d" % (d, i), [128, 2, 128], BF16) for i in range(2)] for d in range(2)]
            CTt = [[S.sb(es, "CTt%d_%d" % (d, i), [128, 2, 128], BF16) for i in range(2)] for d in range(2)]
            btm = [[S.sb(es, "sbtm%d_%d" % (d, i), [128, 256], BF16) for i in range(2)] for d in range(2)]
            xtm = [[S.sb(es, "sxtm%d_%d" % (d, i), [128, 8, 64], BF16) for i in range(2)] for d in range(2)]
            rb8 = [[S.sb(es, "rb8_%d_%d" % (d, i), [128, 8, 128], F32) for i in range(2)] for d in range(2)]
            er8 = [S.sb(es, "er8_%d" % d, [128, 8, 128], F32) for d in range(2)]
            Mh = [S.sb(es, "Mh%d" % d, [128, 8, 128], BF16) for d in range(2)]
            xdt = [S.sb(es, "xdt%d" % d, [128, 8, 64], BF16) for d in range(2)]
            xde = [S.sb(es, "xde%d" % d, [128, 8, 64], BF16) for d in range(2)]
            yo = [S.sb(es, "yo%d" % d, [128, 8, 64], F32) for d in range(2)]
            yst = [[S.sb(es, "yst%d_%d" % (d, i), [128, 512], F32) for i in range(2)] for d in range(2)]
            order_f = list(range(NT))
            order_b = [1, 0] + list(range(NT - 1, NTC - 1, -1))
            orders = [order_f, order_b]
            for step in range(NT):
                for d in range(2):
                    t = orders[d][step]
                    tk = slice(t * 128, (t + 1) * 128)
                    isx = t >= NTC
                    i2 = step % 2
                    bt, ct, bm, xm, rb = BTt[d][i2], CTt[d][i2], btm[d][i2], xtm[d][i2], rb8[d][i2]
                    c8 = slice(8 + d * 8, 16 + d * 8)
                    S.dma("sp", bt.t[:], BT.t[:, :, tk].rearrange("g p k -> p g k"), reads=[BT], writes=[bt])
                    S.dma("sp", ct.t[:], CT.t[:, :, tk].rearrange("g p k -> p g k"), reads=[CT], writes=[ct])
                    S.dma("sp", bm.t[:], Btm.t[tk, :], reads=[Btm], writes=[bm])
                    S.dma("sp", xm.t[:].rearrange("p h k -> p (h k)"), Xtm.t[tk, :], reads=[Xtm], writes=[xm])
                    S.op("pool", lambda e, d=d, xm=xm, t=t, c8=c8: e.tensor_tensor(out=xdt[d].t[:], in0=xm.t[:], in1=bc3(g_sp.t[:, t, c8], 64), op=ALU.mult),
                         reads=[xm, g_sp], writes=[xdt[d]])
                    S.op("pool", lambda e, d=d, t=t, c8=c8: e.tensor_tensor(out=xde[d].t[:], in0=xdt[d].t[:], in1=bc3(g_ekd.t[:, t, c8], 64), op=ALU.mult),
                         reads=[xdt[d], g_ekd], writes=[xde[d]])
                    if isx:
                        S.dma("sp", rb.t[:], GCT.t[t * 24 + 8 + d * 8:t * 24 + 16 + d * 8, :].partition_broadcast(128), reads=[GCT], writes=[rb])
                        for g in range(2):
                            S.op("pe", lambda e, d=d, g=g, bt=bt, ct=ct: e.matmul(pbv[d][:, g * 128:(g + 1) * 128], lhsT=bt.t[:, g, :], rhs=ct.t[:, g, :], start=True, stop=True),
                                 reads=[bt, ct], writes=[pb[d]], inc=(g == 1))
                        S.op("dve", lambda e, d=d, rb=rb, t=t, c8=c8: e.tensor_tensor(out=er8[d].t[:], in0=rb.t[:], in1=bc3(g_gc.t[:, t, c8], 128), op=ALU.subtract),
                             reads=[rb, g_gc], writes=[er8[d]])
                        S.op("dve", lambda e, d=d: e.tensor_tensor(out=er8[d].t[:], in0=er8[d].t[:], in1=msk16.t[:, d * 8:(d + 1) * 8, :], op=ALU.add),
                             reads=[er8[d], msk16], writes=[er8[d]])
                        S.op("act", lambda e, d=d: e.activation(out=er8[d].t[:], in_=er8[d].t[:], func=AF.Exp), reads=[er8[d]], writes=[er8[d]])
                        for g in range(2):
                            S.op("dve", lambda e, d=d, g=g: e.tensor_tensor(out=Mh[d].t[:, g * 4:(g + 1) * 4, :], in0=er8[d].t[:, g * 4:(g + 1) * 4, :],
                                                                           in1=pbv[d][:, g * 128:(g + 1) * 128].unsqueeze(1).to_broadcast([128, 4, 128]), op=ALU.mult),
                                 reads=[er8[d], pb[d]], writes=[Mh[d]])
                        pY, pO = pf[d], pf[2 + d]
                        for h in range(8):
                            S.op("pe", lambda e, d=d, h=h, pY=pY: e.matmul(pY.t[:, h * 64:(h + 1) * 64], lhsT=Mh[d].t[:, h, :], rhs=xdt[d].t[:, h, :], start=True, stop=True),
                                 reads=[Mh[d], xdt[d]], writes=[pY], inc=(h == 7))
                        for g in range(2):
                            S.op("pe", lambda e, d=d, g=g, pO=pO, ct=ct: e.matmul(pO.t[:, g * 256:(g + 1) * 256], lhsT=ct.t[:, g, :], rhs=Hh[d].t[:, g * 256:(g + 1) * 256], start=True, stop=True),
                                 reads=[ct, Hh[d]], writes=[pO], inc=(g == 1))
                        ys = yst[d][i2]
                        S.op("dve", lambda e, d=d, pO=pO, t=t, c8=c8: e.tensor_tensor(out=yo[d].t[:], in0=pO.t[:, :].rearrange("p (h k) -> p h k", h=8),
                                                                                   in1=bc3(g_egc.t[:, t, c8], 64), op=ALU.mult),
                             reads=[pO, g_egc], writes=[yo[d]])
                        S.op("dve", lambda e, d=d, pY=pY, ys=ys: e.tensor_tensor(out=ys.t[:], in0=yo[d].t[:].rearrange("p h k -> p (h k)"), in1=pY.t[:, :], op=ALU.add),
                             reads=[yo[d], pY], writes=[ys])
                        YD = YF if d == 0 else YB
                        S.dma("sp", YD.t[(t - NTC) * 128:(t - NTC + 1) * 128, :], ys.t[:], reads=[ys], writes=[YD])
                    pSt = pf[4 + d]
                    for g in range(2):
                        S.op("pe", lambda e, d=d, g=g, pSt=pSt, bm=bm: e.matmul(pSt.t[:, g * 256:(g + 1) * 256], lhsT=bm.t[:, g * 128:(g + 1) * 128],
                                                                              rhs=xde[d].t[:, g * 4:(g + 1) * 4, :].rearrange("p h k -> p (h k)"), start=True, stop=True),
                             reads=[bm, xde[d]], writes=[pSt], inc=(g == 1))
                    S.op("dve", lambda e, d=d, t=t, c8=c8: e.tensor_tensor(out=Hf[d].t[:], in0=Hf[d].t[:], in1=bc3(g_etot.t[:, t, c8], 64), op=ALU.mult),
                         reads=[Hf[d], g_etot], writes=[Hf[d]])
                    S.op("dve", lambda e, d=d, pSt=pSt: e.tensor_tensor(out=Hf[d].t[:].rearrange("p h k -> p (h k)"), in0=Hf[d].t[:].rearrange("p h k -> p (h k)"),
                                                                       in1=pSt.t[:, :], op=ALU.add), reads=[Hf[d], pSt], writes=[Hf[d]])
                    S.op("act", lambda e, d=d: e.copy(out=Hh[d].t[:], in_=Hf[d].t[:].rearrange("p h k -> p (h k)")), reads=[Hf[d]], writes=[Hh[d]])
            S.barrier()

        wout_v = wout_in.t.rearrange("(kc p) n -> p kc n", p=128)
        with ExitStack() as es:
            wo = S.sb(es, "wo", [128, KC, D], BF16)
            for hf in range(2):
                S.dma("pool", wo.t[:, :, hf * 512:(hf + 1) * 512], wout_v[:, :, hf * 512:(hf + 1) * 512], reads=[wout_in], writes=[wo])
            nrm = S.sb(es, "nrmsb", [128, 3, 512], F32)
            S.dma("sp", nrm.t[:], nrm_in.t[:, :, :], reads=[nrm_in], writes=[nrm])
            of_ = [S.sb(es, "of%d" % i, [128, 512], F32) for i in range(2)]
            ob_ = [S.sb(es, "ob%d" % i, [128, 512], F32) for i in range(2)]
            yf_ = [S.sb(es, "yf%d" % i, [128, 512], F32) for i in range(2)]
            yb_ = [S.sb(es, "yb%d" % i, [128, 512], F32) for i in range(2)]
            zg_ = [S.sb(es, "zg%d" % i, [128, 512], BF16) for i in range(2)]
            zs_ = [S.sb(es, "zs%d" % i, [128, 512], BF16) for i in range(2)]
            xs_ = [S.sb(es, "xs%d" % i, [128, 512], BF16) for i in range(2)]
            xr_ = [S.sb(es, "xr%d" % i, [128, D], F32) for i in range(2)]
            junk5 = S.sb(es, "junk5", [128, 512], BF16)
            st5 = [S.sb(es, "st5_%d" % i, [128, 8], F32) for i in range(2)]
            mixb = [S.sb(es, "mixb%d" % i, [128, D], BF16) for i in range(2)]
            mixT = [S.sb(es, "mixT%d" % i, [128, KC, 128], BF16) for i in range(2)]
            x1t = [S.sb(es, "x1t%d" % i, [128, D], F32) for i in range(2)]
            for tt in range(L // 128):
                i2 = tt % 2
                rs = slice(tt * 128, (tt + 1) * 128)
                of, ob, yf, yb, zg, zs, xs, xr, st, mb, mT, x1 = of_[i2], ob_[i2], yf_[i2], yb_[i2], zg_[i2], zs_[i2], xs_[i2], xr_[i2], st5[i2], mixb[i2], mixT[i2], x1t[i2]
                S.dma("sp", of.t[:], OF.t[rs, :], reads=[OF], writes=[of])
                S.dma("sp", ob.t[:], OB.t[rs, :], reads=[OB], writes=[ob])
                S.dma("sp", yf.t[:], YF.t[rs, :], reads=[YF], writes=[yf])
                S.dma("sp", yb.t[:], YB.t[rs, :], reads=[YB], writes=[yb])
                S.dma("sp", zg.t[:], ZG.t[rs, :], reads=[ZG], writes=[zg])
                S.dma("sp", zs.t[:], ZS.t[rs, :], reads=[ZS], writes=[zs])
                S.dma("sp", xs.t[:], Xtm.t[LC + tt * 128:LC + (tt + 1) * 128, :], reads=[Xtm], writes=[xs])
                S.dma("sp", xr.t[:], x_in.t[rs, :], reads=[x_in], writes=[xr])
                S.op("dve", lambda e, of=of, ob=ob: e.tensor_tensor(out=of.t[:], in0=of.t[:], in1=ob.t[:], op=ALU.add), reads=[of, ob], writes=[of])
                for h in range(4):
                    S.op("act", lambda e, of=of, st=st, h=h: e.activation(out=junk5.t[:, 0:128], in_=of.t[:, h * 128:(h + 1) * 128], func=AF.Square, accum_out=st.t[:, h:h + 1]),
                         reads=[of], writes=[junk5, st])
                S.op("act", lambda e, st=st: e.activation(out=st.t[:, 0:4], in_=st.t[:, 0:4], func=AF.Sqrt, scale=1.0 / 128, bias=epsb.t[:, 0:1]), reads=[st, epsb], writes=[st])
                S.op("dve", lambda e, st=st: e.reciprocal(out=st.t[:, 0:4], in_=st.t[:, 0:4]), reads=[st], writes=[st])
                S.op("dve", lambda e, of=of, st=st: e.tensor_tensor(out=of.t[:].rearrange("p (h k) -> p h k", h=4), in0=of.t[:].rearrange("p (h k) -> p h k", h=4),
                                                                 in1=bc3(st.t[:, 0:4], 128), op=ALU.mult), reads=[of, st], writes=[of])
                S.op("dve", lambda e, of=of: e.tensor_tensor(out=of.t[:], in0=of.t[:], in1=nrm.t[:, 0, :], op=ALU.mult), reads=[of, nrm], writes=[of])
                S.op("dve", lambda e, of=of, zg=zg, mb=mb: e.tensor_tensor(out=mb.t[:, 0:512], in0=of.t[:], in1=zg.t[:], op=ALU.mult), reads=[of, zg], writes=[mb])
                S.op("pool", lambda e, yf=yf, yb=yb: e.tensor_tensor(out=yf.t[:], in0=yf.t[:], in1=yb.t[:], op=ALU.add), reads=[yf, yb], writes=[yf])
                S.op("pool", lambda e, yb=yb, xs=xs: e.tensor_tensor(out=yb.t[:], in0=xs.t[:], in1=nrm.t[:, 2, :], op=ALU.mult), reads=[xs, nrm], writes=[yb])
                S.op("pool", lambda e, yf=yf, yb=yb: e.tensor_tensor(out=yf.t[:], in0=yf.t[:], in1=yb.t[:], op=ALU.add), reads=[yf, yb], writes=[yf])
                S.op("pool", lambda e, yf=yf, zs=zs: e.tensor_tensor(out=yf.t[:], in0=yf.t[:], in1=zs.t[:], op=ALU.mult), reads=[yf, zs], writes=[yf])
                S.op("act", lambda e, yf=yf, st=st: e.activation(out=junk5.t[:], in_=yf.t[:], func=AF.Square, accum_out=st.t[:, 4:5]), reads=[yf], writes=[junk5, st])
                S.op("act", lambda e, st=st: e.activation(out=st.t[:, 4:5], in_=st.t[:, 4:5], func=AF.Sqrt, scale=1.0 / 512, bias=epsb.t[:, 0:1]), reads=[st, epsb], writes=[st])
                S.op("dve", lambda e, st=st: e.reciprocal(out=st.t[:, 4:5], in_=st.t[:, 4:5]), reads=[st], writes=[st])
                S.op("dve", lambda e, yf=yf, st=st, mb=mb: e.scalar_tensor_tensor(out=mb.t[:, 512:1024], in0=yf.t[:], scalar=st.t[:, 4:5], in1=nrm.t[:, 1, :], op0=ALU.mult, op1=ALU.mult),
                     reads=[yf, st, nrm], writes=[mb])
                pt = pb[i2]
                for kc in range(KC):
                    S.op("pe", lambda e, kc=kc, mb=mb, pt=pt: e.transpose(out=pt.t[:, kc * 128:(kc + 1) * 128], in_=mb.t[:, kc * 128:(kc + 1) * 128], identity=identb),
                         reads=[mb, cstb], writes=[pt], inc=(kc == KC - 1))
                S.op("act", lambda e, pt=pt, mT=mT: e.copy(out=mT.t[:].rearrange("p a b -> p (a b)"), in_=pt.t[:, :]), reads=[pt], writes=[mT])
                for hf in range(2):
                    p = pf[(tt * 2 + hf) % 4]
                    for kc in range(KC):
                        S.op("pe", lambda e, p=p, kc=kc, mT=mT, hf=hf: e.matmul(p.t[:, :], lhsT=mT.t[:, kc, :], rhs=wo.t[:, kc, hf * 512:(hf + 1) * 512],
                                                                              start=(kc == 0), stop=(kc == KC - 1)), reads=[mT, wo], writes=[p], inc=(kc == KC - 1))
                    S.op("dve", lambda e, p=p, x1=x1, hf=hf: e.tensor_tensor(out=x1.t[:, hf * 512:(hf + 1) * 512], in0=p.t[:, :], in1=g1bc.t[:, hf * 512:(hf + 1) * 512], op=ALU.mult),
                         reads=[p, g1bc], writes=[x1])
                S.op("pool", lambda e, x1=x1, xr=xr: e.tensor_tensor(out=x1.t[:], in0=x1.t[:], in1=xr.t[:], op=ALU.add), reads=[x1, xr], writes=[x1])
                S.dma("sp", X1.t[rs, :], x1.t[:], reads=[x1], writes=[X1])
            S.barrier()

        with ExitStack() as es:
            def src2(t):
                return X1.t[t * 128:(t + 1) * 128, :]

            def cfg2(t):
                return 2, (lambda kc: modfm.t[:, 16 + kc, 0:1]), t * 128
            norm_to_T(es, src2, L // 128, cfg2, hT, X1)
            S.barrier()
        wg_v = wgate_in.t.rearrange("(kc p) n -> p kc n", p=128)
        wu_v = wup_in.t.rearrange("(kc p) n -> p kc n", p=128)
        with ExitStack() as es:
            fcw = S.sb(es, "fcw", [128, FC, 9], F32)
            fcb = S.sb(es, "fcb", [128, FC], F32)
            S.dma("sp", fcw.t[:], fcw_in.t[:, :, :], reads=[fcw_in], writes=[fcw])
            S.dma("sp", fcb.t[:], fcb_in.t[:, :], reads=[fcb_in], writes=[fcb])
            wgt = [S.sb(es, "wgt%d" % i, [128, KC, 128], BF16) for i in range(2)]
            wut = [S.sb(es, "wut%d" % i, [128, KC, 128], BF16) for i in range(2)]
            gimg = [S.sb(es, "gimg%d" % i, [128, 66, 66], BF16) for i in range(2)]
            dg9 = [S.sb(es, "dg9_%d" % i, [128, 9, 128], BF16) for i in range(2)]
            asil = [S.sb(es, "asil%d" % i, [128, 512], F32) for i in range(2)]
            midb = [S.sb(es, "midb%d" % i, [128, 512], BF16) for i in range(3)]
            for gb in gimg:
                S.op("pool", lambda e, gb=gb: e.memset(gb.t[:], 0.0), writes=[gb])
            mc = 0
            for fc in range(FC):
                i2 = fc % 2
                wgb, wub, gb, d9 = wgt[i2], wut[i2], gimg[i2], dg9[i2]
                S.dma("pool", wgb.t[:], wg_v[:, :, fc * 128:(fc + 1) * 128], reads=[wgate_in], writes=[wgb])
                S.dma("pool", wub.t[:], wu_v[:, :, fc * 128:(fc + 1) * 128], reads=[wup_in], writes=[wub])
                for tap in range(9):
                    S.op("dve", lambda e, d9=d9, tap=tap, fc=fc: e.tensor_scalar(out=d9.t[:, tap, :], in0=identb, scalar1=fcw.t[:, fc, tap:tap + 1], scalar2=None, op0=ALU.mult),
                         reads=[cstb, fcw], writes=[d9])
                for b in range(8):
                    p = pf[b % 2]
                    for kc in range(KC):
                        S.op("pe", lambda e, p=p, kc=kc, b=b, wgb=wgb: e.matmul(p.t[:, :], lhsT=wgb.t[:, kc, :], rhs=hT.t[:, kc, b * 512:(b + 1) * 512],
                                                                              start=(kc == 0), stop=(kc == KC - 1)), reads=[wgb, hT], writes=[p], inc=(kc == KC - 1))
                    S.op("act", lambda e, p=p, gb=gb, b=b: e.copy(out=gb.t[:, 1 + 8 * b:9 + 8 * b, 1:65], in_=p.t[:, :].rearrange("p (r c) -> p r c", c=64)),
                         reads=[p], writes=[gb])
                for b in range(8):
                    p = pf[2 + b % 2]
                    for tap in range(9):
                        dr, dc = tap // 3, tap % 3
                        S.op("pe", lambda e, p=p, tap=tap, dr=dr, dc=dc, b=b, d9=d9, gb=gb: e.matmul(
                            p.t[:, :].rearrange("p (r c) -> p r c", c=64), lhsT=d9.t[:, tap, :], rhs=gb.t[:, 8 * b + dr:8 * b + dr + 8, dc:dc + 64],
                            start=(tap == 0), stop=(tap == 8)), reads=[d9, gb], writes=[p], inc=(tap == 8))
                    a = asil[b % 2]
                    S.op("act", lambda e, p=p, a=a, fc=fc: e.activation(out=a.t[:], in_=p.t[:, :], func=AF.Silu, bias=fcb.t[:, fc:fc + 1]), reads=[p, fcb], writes=[a])
                    p2 = pf[4 + b % 2]
                    for kc in range(KC):
                        S.op("pe", lambda e, p2=p2, kc=kc, b=b, wub=wub: e.matmul(p2.t[:, :], lhsT=wub.t[:, kc, :], rhs=hT.t[:, kc, b * 512:(b + 1) * 512],
                                                                                start=(kc == 0), stop=(kc == KC - 1)), reads=[wub, hT], writes=[p2], inc=(kc == KC - 1))
                    mb = midb[mc % 3]
                    mc += 1
                    S.op("dve", lambda e, p2=p2, a=a, mb=mb: e.tensor_tensor(out=mb.t[:], in0=a.t[:], in1=p2.t[:, :], op=ALU.mult), reads=[a, p2], writes=[mb])
                    S.dma("sp", MID.t[fc, :, b * 512:(b + 1) * 512], mb.t[:], reads=[mb], writes=[MID])
            S.barrier()
        wd_v = wdown_in.t.rearrange("(fc p) n -> p fc n", p=128)
        with ExitStack() as es:
            wd = S.sb(es, "wd", [128, FC, D], BF16)
            for hf in range(2):
                for q4 in range(0, FC, 2):
                    S.dma("pool", wd.t[:, q4:q4 + 2, hf * 512:(hf + 1) * 512], wd_v[:, q4:q4 + 2, hf * 512:(hf + 1) * 512], reads=[wdown_in], writes=[wd])
            fng = S.sb(es, "fng", [128, D], F32)
            S.dma("sp", fng.t[:], fng_in.t[:, :], reads=[fng_in], writes=[fng])
            midt = [S.sb(es, "midt%d" % i, [128, FC, 128], BF16) for i in range(2)]
            x1r = [S.sb(es, "x1r%d" % i, [128, D], F32) for i in range(2)]
            x2t = [S.sb(es, "x2t%d" % i, [128, D], F32) for i in range(2)]
            junk6 = S.sb(es, "junk6", [128, D], BF16)
            st6 = [S.sb(es, "st6_%d" % i, [128, 2], F32) for i in range(2)]
            for tt in range(L // 128):
                i2 = tt % 2
                rs = slice(tt * 128, (tt + 1) * 128)
                mt, xr, x2, st = midt[i2], x1r[i2], x2t[i2], st6[i2]
                S.dma("sp", mt.t[:], MID.t[:, :, rs].rearrange("f p k -> p f k"), reads=[MID], writes=[mt])
                S.dma("sp", xr.t[:], X1.t[rs, :], reads=[X1], writes=[xr])
                for hf in range(2):
                    p = pf[(tt * 2 + hf) % 4]
                    for fc in range(FC):
                        S.op("pe", lambda e, p=p, fc=fc, mt=mt, hf=hf: e.matmul(p.t[:, :], lhsT=mt.t[:, fc, :], rhs=wd.t[:, fc, hf * 512:(hf + 1) * 512],
                                                                              start=(fc == 0), stop=(fc == FC - 1)), reads=[mt, wd], writes=[p], inc=(fc == FC - 1))
                    S.op("dve", lambda e, p=p, x2=x2, hf=hf: e.tensor_tensor(out=x2.t[:, hf * 512:(hf + 1) * 512], in0=p.t[:, :], in1=g2bc.t[:, hf * 512:(hf + 1) * 512], op=ALU.mult),
                         reads=[p, g2bc], writes=[x2])
                S.op("pool", lambda e, x2=x2, xr=xr: e.tensor_tensor(out=x2.t[:], in0=x2.t[:], in1=xr.t[:], op=ALU.add), reads=[x2, xr], writes=[x2])
                S.op("act", lambda e, x2=x2, st=st: e.activation(out=junk6.t[:], in_=x2.t[:], func=AF.Square, accum_out=st.t[:, 0:1]), reads=[x2], writes=[junk6, st])
                S.op("act", lambda e, st=st: e.activation(out=st.t[:, 1:2], in_=st.t[:, 0:1], func=AF.Sqrt, scale=1.0 / D, bias=epsb.t[:, 0:1]), reads=[st, epsb], writes=[st])
                S.op("dve", lambda e, st=st: e.reciprocal(out=st.t[:, 0:1], in_=st.t[:, 1:2]), reads=[st], writes=[st])
                S.op("dve", lambda e, x2=x2, st=st: e.scalar_tensor_tensor(out=x2.t[:], in0=x2.t[:], scalar=st.t[:, 0:1], in1=fng.t[:], op0=ALU.mult, op1=ALU.mult),
                     reads=[x2, st, fng], writes=[x2])
                S.dma("sp", y_out.t[rs, :], x2.t[:], reads=[x2], writes=[y_out])
            S.barrier()

        S.barrier()

        with nc.Block() as block:
            @block.sync
            def _(eng):
                S.replay("sp", eng)

            @block.tensor
            def _(eng):
                S.replay("pe", eng)

            @block.vector
            def _(eng):
                S.replay("dve", eng)

            @block.scalar
            def _(eng):
                S.replay("act", eng)

            @block.gpsimd
            def _(eng):
                S.replay("pool", eng)
        print("n_ops", S.n_ops, {e: len(S.streams[e]) for e in ENGS})
    return nc


def host_inputs(inputs, b):
    f = np.float32
    c = np.asarray(inputs["c"][b], f)
    cc = np.asarray(inputs["c_ctx"], f)
    cT = np.stack([c.reshape(KC, 128).T, cc.reshape(KC, 128).T], axis=-1)
    b_ada = np.asarray(inputs["b_ada"][0], f)
    bch = b_ada.reshape(48, 128)
    sel = list(range(0, 16)) + list(range(24, 40))
    b_fm = bch[sel].T.copy()
    b_row = np.concatenate([b_ada[2048:3072], b_ada[5120:6144]])[None, :]
    gcw = np.asarray(inputs["gdn_conv_w"][0], f)
    scw = np.asarray(inputs["ssm_conv_w"][0], f)
    cw = np.concatenate([gcw, scw], axis=1)
    cw_fm = cw.reshape(3, 20, 128).transpose(2, 1, 0).copy()
    cb = np.concatenate([np.zeros(1536, f), np.asarray(inputs["ssm_conv_b"][0], f)])
    cb_fm = cb.reshape(20, 128).T.copy()
    ident = np.eye(128, dtype=f)
    j = np.arange(128)[:, None]
    i = np.arange(128)[None, :]
    trif = (j <= i).astype(f)
    trib = (j >= i).astype(f)
    ones = np.ones((128, 128), f)
    consts = np.concatenate([ident, trif, trib, ones, np.zeros((128, 512), f)], axis=1)
    gb = np.concatenate([np.asarray(inputs["gdn_dt_bias"][0], f).reshape(8), np.asarray(inputs["ssm_dt_bias"][0], f).reshape(16)])
    ga = np.concatenate([np.asarray(inputs["gdn_A_log"][0], f).reshape(8), np.asarray(inputs["ssm_A_log"][0], f).reshape(16)])
    NEG = np.float32(-1.0e4)
    pq = np.arange(128)[:, None]
    fq = np.arange(128)[None, :]
    mq_f = np.where(pq <= fq, 0.0, NEG).astype(f)
    mq_b = np.where(pq >= fq, 0.0, NEG).astype(f)
    mk_f = np.where(fq < pq, 0.0, NEG).astype(f)
    mk_b = np.where(fq > pq, 0.0, NEG).astype(f)
    masks = np.stack([mq_f] * 4 + [mq_b] * 4 + [mk_f] * 4 + [mk_b] * 4 + [mq_f] * 8 + [mq_b] * 8, axis=1)
    gate_bias = np.broadcast_to(gb[None, None, :], (128, NT, 24)).copy()
    gate_alog = np.broadcast_to(ga[None, None, :], (128, NT, 24)).copy()
    return {
        "x": np.ascontiguousarray(inputs["x"][b], f),
        "ctx": np.ascontiguousarray(inputs["ctx"][b], f),
        "cT": np.ascontiguousarray(cT, f),
        "w_ada": np.ascontiguousarray(inputs["w_ada"][0], f),
        "b_ada_fm": b_fm, "b_ada_row": np.ascontiguousarray(b_row, f),
        "n1g_fm": np.asarray(inputs["norm1_g"][0], f).reshape(KC, 128).T.copy(),
        "n2g_fm": np.asarray(inputs["norm2_g"][0], f).reshape(KC, 128).T.copy(),
        "w_in": np.ascontiguousarray(inputs["w_in"][0], f),
        "convw_fm": cw_fm, "convb_fm": cb_fm, "consts": consts,
        "gate_bias": gate_bias, "gate_alog": gate_alog, "masks": np.ascontiguousarray(masks, f),
        "w_out": np.ascontiguousarray(inputs["w_out"][0], f),
        "nrm": np.ascontiguousarray(np.broadcast_to(np.stack([np.tile(np.asarray(inputs["gdn_norm_g"][0], f), 4), np.asarray(inputs["ssm_norm_g"][0], f),
                                                              np.repeat(np.asarray(inputs["ssm_D"][0], f), 64)])[None], (128, 3, 512)), f),
        "w_gate": np.ascontiguousarray(inputs["ffn_w_gate"][0], f), "w_up": np.ascontiguousarray(inputs["ffn_w_up"][0], f),
        "w_down": np.ascontiguousarray(inputs["ffn_w_down"][0], f),
        "fcw_fm": np.ascontiguousarray(np.asarray(inputs["ffn_conv_w"][0], f).reshape(9, FC, 128).transpose(2, 1, 0), f),
        "fcb_fm": np.ascontiguousarray(np.asarray(inputs["ffn_conv_b"][0], f).reshape(FC, 128).T, f),
        "fng_bc": np.ascontiguousarray(np.broadcast_to(np.asarray(inputs["final_norm_g"], f)[None], (128, D)), f),
    }


def kernel(**inputs):
    nc = build()
    in_maps = [host_inputs(inputs, b) for b in range(8)]
    res = run_bass_kernel_spmd(nc, in_maps, core_ids=list(range(8)))
    return np.stack([np.asarray(r["y"], np.float32) for r in res.results], axis=0)
```

```python
import numpy as np
from contextlib import ExitStack
import concourse.bass as bass
import concourse.mybir as mybir
from concourse.bass_utils import run_bass_kernel_spmd

F32 = mybir.dt.float32
BF16 = mybir.dt.bfloat16
AF = mybir.ActivationFunctionType
ALU = mybir.AluOpType

D = 1024
KC = 8
L = 4096
LC = 256
LT = L + LC
NT = LT // 128
NTC = LC // 128
DFF = 2816
FC = DFF // 128
EPS = 1e-6
KD = 8

ENGS = ("pe", "dve", "act", "pool", "sp")


class Buf:
    __slots__ = ("name", "t", "last_w", "reads")

    def __init__(self, name, t):
        self.name = name
        self.t = t
        self.last_w = None
        self.reads = {}

    def __getitem__(self, k):
        return self.t[k]


class Sched:
    def __init__(self, nc, es):
        self.nc = nc
        self.es = es
        self.streams = {e: [] for e in ENGS}
        self.semh = {}
        self.cnt = {e: 0 for e in ENGS}
        self.seen = {e: {} for e in ENGS}
        for e in ("pe", "dve", "act", "pool"):
            self.semh[e] = es.enter_context(nc.semaphore("s_" + e))
        self.dma_n = {}
        for q in ("sp", "pool", "act"):
            self.dma_n[q] = 0
            for i in range(KD):
                nm = "d_%s%d" % (q, i)
                self.semh[nm] = es.enter_context(nc.semaphore(nm))
        self.n_ops = 0

    def sb(self, es, name, shape, dt):
        self.n_sb = getattr(self, "n_sb", 0) + 1
        name = "%s_u%d" % (name, self.n_sb)
        return Buf(name, es.enter_context(self.nc.sbuf_tensor(name, list(shape), dt)))

    def ps(self, es, name, shape, dt):
        return Buf(name, es.enter_context(self.nc.psum_tensor(name, list(shape), dt)))

    def dram(self, name, shape, dt, kind="Internal"):
        return Buf(name, self.nc.dram_tensor(name, list(shape), dt, kind=kind).ap())

    def _deps(self, reads, writes):
        deps = {}

        def add(k, v):
            if deps.get(k, 0) < v:
                deps[k] = v
        for b in reads:
            if b.last_w is not None:
                add(*b.last_w)
        for b in writes:
            if b.last_w is not None:
                add(*b.last_w)
            for k, v in b.reads.items():
                add(k, v)
        return deps

    def _waits(self, e, deps):
        for k, v in deps.items():
            if k == e and e == "pe":
                continue
            if self.seen[e].get(k, 0) >= v:
                continue
            self.streams[e].append(("wait", k, v))
            self.seen[e][k] = v

    def _record(self, ev, reads, writes):
        k, v = ev
        for b in reads:
            if b.reads.get(k, 0) < v:
                b.reads[k] = v
        for b in writes:
            b.last_w = ev
            b.reads = {}

    def op(self, e, fn, reads=(), writes=(), inc=True):
        self.n_ops += 1
        self._waits(e, self._deps(reads, writes))
        if inc:
            self.cnt[e] += 1
            ev = (e, self.cnt[e])
            self.streams[e].append(("op", fn, True))
        else:
            ev = (e, self.cnt[e] + 1)
            self.streams[e].append(("op", fn, False))
        self._record(ev, reads, writes)

    def dma(self, q, out_ap, in_ap, reads=(), writes=(), **kw):
        self.n_ops += 1
        n = self.dma_n[q]
        self.dma_n[q] += 1
        nm = "d_%s%d" % (q, n % KD)
        val = 16 * (n // KD + 1)
        deps = self._deps(reads, writes)
        if n >= KD and deps.get(nm, 0) < val - 16:
            deps[nm] = val - 16
        self._waits(q, deps)
        self.streams[q].append(("dma", out_ap, in_ap, nm, kw))
        self._record((nm, val), reads, writes)

    def barrier(self):
        tot = {}
        for e in ("pe", "dve", "act", "pool"):
            if self.cnt[e] > 0:
                tot[e] = self.cnt[e]
        for q in ("sp", "pool", "act"):
            n = self.dma_n[q]
            for i in range(min(n, KD)):
                cntd = (n - 1 - i) // KD + 1
                tot["d_%s%d" % (q, i)] = 16 * cntd
        for e in ENGS:
            d = dict(tot)
            self._waits_all(e, d)

    def _waits_all(self, e, deps):
        for k, v in deps.items():
            if self.seen[e].get(k, 0) >= v:
                continue
            self.streams[e].append(("wait", k, v))
            self.seen[e][k] = v

    def replay(self, e, eng):
        for it in self.streams[e]:
            if it[0] == "wait":
                eng.wait_ge(self.semh[it[1]], it[2])
            elif it[0] == "op":
                ins = it[1](eng)
                if it[2]:
                    ins.then_inc(self.semh[e], 1)
            else:
                _, o, i, nm, kw = it
                eng.dma_start(out=o, in_=i, **kw).then_inc(self.semh[nm], 16)


C_Q, C_K, C_V, C_ZG, C_A, C_B, C_ZS, C_XS, C_BS, C_CS, C_DT = 0, 512, 1024, 1536, 2048, 2056, 2064, 2576, 3088, 3344, 3600


def post_block(S, gname, ci, a16, t0, ntk, cnt, tmst, pb, identb, cstb, DR):
    fm_dst = None
    tm_dst = None
    if gname == "q":
        fm_dst = DR["QT"].t[ci, :, t0:t0 + ntk]; fmb = DR["QT"]
    elif gname == "k":
        fm_dst = DR["KT"].t[ci, :, t0:t0 + ntk]; fmb = DR["KT"]
        tm_dst = (DR["Ktm"], ci * 128)
    elif gname == "v":
        tm_dst = (DR["Vtm"], ci * 128)
    elif gname == "xs":
        tm_dst = (DR["Xtm"], ci * 128)
    elif gname == "bc":
        if ci < 2:
            fm_dst = DR["BT"].t[ci, :, t0:t0 + ntk]; fmb = DR["BT"]
            tm_dst = (DR["Btm"], ci * 128)
        else:
            fm_dst = DR["CT"].t[ci - 2, :, t0:t0 + ntk]; fmb = DR["CT"]
    if fm_dst is not None:
        S.dma("sp", fm_dst, a16.t[:, 0:ntk], reads=[a16], writes=[fmb])
    if tm_dst is not None:
        dbuf, c0 = tm_dst
        nsub = ntk // 128
        pt = pb[cnt["tm"] % 2]
        st = tmst[cnt["tm"] % 2]
        cnt["tm"] += 1
        for sidx in range(nsub):
            S.op("pe", lambda e, pt=pt, sidx=sidx, a16=a16: e.transpose(out=pt.t[:, sidx * 128:(sidx + 1) * 128],
                                                                       in_=a16.t[:, sidx * 128:(sidx + 1) * 128], identity=identb),
                 reads=[a16, cstb], writes=[pt], inc=(sidx == nsub - 1))
        S.op("dve", lambda e, pt=pt, st=st, nsub=nsub: e.tensor_copy(out=st.t[:, 0:nsub, :].rearrange("p a b -> p (a b)"), in_=pt.t[:, 0:nsub * 128]),
             reads=[pt], writes=[st])
        S.dma("sp", dbuf.t[t0:t0 + ntk, c0:c0 + 128].rearrange("(a p) c -> p a c", p=128), st.t[:, 0:nsub, :], reads=[st], writes=[dbuf])


import os as _os
GDN_ON = True
RUN3B = _os.environ.get('RUN3B', '1') == '1'
SSD_ON = True


def build(debug=None):
    nc = bass.Bass("TRN2", target_bir_lowering=False)
    dbg = {}

    def ext_in(name, shape, dt=F32):
        return Buf(name, nc.dram_tensor(name, list(shape), dt, kind="ExternalInput").ap())

    x_in = ext_in("x", [L, D])
    ctx_in = ext_in("ctx", [LC, D])
    cT_in = ext_in("cT", [128, KC, 2])
    wada_in = ext_in("w_ada", [D, 6 * D])
    bfm_in = ext_in("b_ada_fm", [128, 32])
    brow_in = ext_in("b_ada_row", [1, 2048])
    n1g_in = ext_in("n1g_fm", [128, KC])
    n2g_in = ext_in("n2g_fm", [128, KC])
    win_in = ext_in("w_in", [D, 3616])
    cw_in = ext_in("convw_fm", [128, 20, 3])
    cb_in = ext_in("convb_fm", [128, 20])
    cst_in = ext_in("consts", [128, 1024])
    gcb_in = ext_in("gate_bias", [128, NT, 24])
    gca_in = ext_in("gate_alog", [128, NT, 24])
    msk_in = ext_in("masks", [128, 32, 128])
    wout_in = ext_in("w_out", [D, D])
    nrm_in = ext_in("nrm", [128, 3, 512])
    wgate_in = ext_in("w_gate", [D, DFF])
    wup_in = ext_in("w_up", [D, DFF])
    wdown_in = ext_in("w_down", [DFF, D])
    fcw_in = ext_in("fcw_fm", [128, FC, 9])
    fcb_in = ext_in("fcb_fm", [128, FC])
    fng_in = ext_in("fng_bc", [128, D])
    y_out = Buf("y", nc.dram_tensor("y", [L, D], F32, kind="ExternalOutput").ap())

    with ExitStack() as es0:
        S = Sched(nc, es0)

        def dkind(name):
            return "ExternalOutput" if (debug and name in debug) else "Internal"

        def scratch(name, shape, dt):
            b = S.dram(name, shape, dt, kind=dkind(name))
            return b

        QT = scratch("QT", [4, 128, LT], BF16)
        KT = scratch("KT", [4, 128, LT], BF16)
        Ktm = scratch("Ktm", [LT, 512], BF16)
        Vtm = scratch("Vtm", [LT, 512], BF16)
        Xtm = scratch("Xtm", [LT, 512], BF16)
        BT = scratch("BT", [2, 128, LT], BF16)
        CT = scratch("CT", [2, 128, LT], BF16)
        Btm = scratch("Btm", [LT, 256], BF16)
        ZG = scratch("ZG", [L, 512], BF16)
        ZS = scratch("ZS", [L, 512], BF16)
        GCT = scratch("GCT", [NT * 24, 128], F32)
        UWQ = scratch("UWQ", [NT, 2, 4, 128, 512], BF16)
        OF = scratch("OF", [L, 512], F32)
        OB = scratch("OB", [L, 512], F32)
        YF = scratch("YF", [L, 512], F32)
        YB = scratch("YB", [L, 512], F32)
        X1 = scratch("X1", [L, D], F32)
        MID = scratch("MID", [FC, 128, L], BF16)
        HTD = scratch("HTD", [128, KC, LT], BF16)
        GD = scratch("GD", [128, NT, 104], F32)
        MODD = scratch("MODD", [128, 64 + 2 * D + 24], F32)

        cst = S.sb(es0, "cst", [128, 1024], F32)
        identf = cst.t[:, 0:128]
        trif = cst.t[:, 128:256]
        trib = cst.t[:, 256:384]
        onesf = cst.t[:, 384:512]
        cstb = S.sb(es0, "cstb", [128, 512], BF16)
        identb = cstb.t[:, 0:128]
        onesb = cstb.t[:, 384:512]
        modfm = S.sb(es0, "modfm", [128, 32, 2], F32)
        g1bc = S.sb(es0, "g1bc", [128, D], F32)
        g2bc = S.sb(es0, "g2bc", [128, D], F32)
        AB = S.sb(es0, "AB", [128, 3, KC], F32)
        n1g = S.sb(es0, "n1g", [128, KC], F32)
        n2g = S.sb(es0, "n2g", [128, KC], F32)
        hT = S.sb(es0, "hT", [128, KC, LT], BF16)
        graw = S.sb(es0, "graw", [128, NT, 32], F32)
        g_sp = S.sb(es0, "g_sp", [128, NT, 24], F32)
        g_g = S.sb(es0, "g_g", [128, NT, 24], F32)
        g_gc = S.sb(es0, "g_gc", [128, NT, 24], F32)
        g_tot = S.sb(es0, "g_tot", [128, NT, 24], F32)
        g_beta = S.sb(es0, "g_beta", [128, NT, 8], F32)
        g_egc = S.sb(es0, "g_egc", [128, NT, 24], F32)
        g_ekd = S.sb(es0, "g_ekd", [128, NT, 24], F32)
        g_etot = S.sb(es0, "g_etot", [128, NT, 24], F32)
        g_nbeta = S.sb(es0, "g_nbeta", [128, NT, 8], F32)
        g_bek = S.sb(es0, "g_bek", [128, NT, 8], F32)

        epsb = S.sb(es0, "epsb", [128, 4], F32)
        S.op("dve", lambda e: e.memset(epsb.t[:, 0:1], EPS), writes=[epsb])
        S.op("dve", lambda e: e.memset(epsb.t[:, 1:2], 4.0 * EPS), writes=[epsb])
        S.op("dve", lambda e: e.memset(epsb.t[:, 2:3], 1.0), writes=[epsb])
        S.op("dve", lambda e: e.memset(epsb.t[:, 3:4], 0.0), writes=[epsb])
        S.dma("sp", cst.t[:], cst_in.t[:, :], reads=[cst_in], writes=[cst])
        S.dma("sp", n1g.t[:], n1g_in.t[:, :], reads=[n1g_in], writes=[n1g])
        S.dma("sp", n2g.t[:], n2g_in.t[:, :], reads=[n2g_in], writes=[n2g])
        S.op("dve", lambda e: e.tensor_copy(out=cstb.t[:], in_=cst.t[:, 0:512]), reads=[cst], writes=[cstb])

        pf = [S.ps(es0, "pf%d" % i, [128, 512], F32) for i in range(6)]
        pb = [S.ps(es0, "pb%d" % i, [128, 1024], BF16) for i in range(2)]

        with ExitStack() as es:
            c_sb = S.sb(es, "c_sb", [128, KC, 2], F32)
            sc = S.sb(es, "sc", [128, KC, 2], F32)
            cbc = S.sb(es, "cbc", [128, KC, 128], F32)
            bfm = S.sb(es, "bfm", [128, 32], F32)
            brow = S.sb(es, "brow", [1, 2048], F32)
            wa = [S.sb(es, "wa%d" % i, [128, KC, 512], F32) for i in range(2)]
            S.dma("sp", c_sb.t[:], cT_in.t[:, :, :], reads=[cT_in], writes=[c_sb])
            S.dma("sp", bfm.t[:], bfm_in.t[:, :], reads=[bfm_in], writes=[bfm])
            S.dma("sp", brow.t[:], brow_in.t[:, :], reads=[brow_in], writes=[brow])
            S.op("act", lambda e: e.activation(out=sc.t[:], in_=c_sb.t[:], func=AF.Silu), reads=[c_sb], writes=[sc])
            for kc in range(KC):
                S.op("dve", lambda e, kc=kc: e.tensor_scalar(out=cbc.t[:, kc, :], in0=onesf, scalar1=sc.t[:, kc, 0:1],
                                                           scalar2=None, op0=ALU.mult), reads=[sc, cst], writes=[cbc])
            wada_v = wada_in.t.rearrange("(kc p) n -> p kc n", p=128)
            fm_tiles = [0, 1, 2, 3, 6, 7, 8, 9]
            order = [0, 1, 2, 3, 4, 5, 6, 7, 8, 9, 10, 11]
            for i, n in enumerate(order):
                w = wa[i % 2]
                S.dma("sp", w.t[:], wada_v[:, :, n * 512:(n + 1) * 512], reads=[wada_in], writes=[w])
                if n in fm_tiles:
                    base = fm_tiles.index(n) * 4
                    for f in range(4):
                        for kc in range(KC):
                            S.op("pe", lambda e, w=w, f=f, kc=kc, base=base: e.matmul(
                                pf[0].t[:, (base + f) * 2:(base + f) * 2 + 2], lhsT=w.t[:, kc, f * 128:(f + 1) * 128],
                                rhs=sc.t[:, kc, :], start=(kc == 0), stop=(kc == KC - 1)),
                                reads=[w, sc], writes=[pf[0]], inc=(kc == KC - 1))
                else:
                    j = [4, 5, 10, 11].index(n)
                    pbk = pf[1 + (j % 2)]
                    for kc in range(KC):
                        S.op("pe", lambda e, w=w, kc=kc, pbk=pbk: e.matmul(
                            pbk.t[:, :], lhsT=cbc.t[:, kc, :], rhs=w.t[:, kc, :], start=(kc == 0), stop=False),
                            reads=[w, cbc], writes=[pbk], inc=False)
                    S.op("pe", lambda e, pbk=pbk, j=j: e.matmul(
                        pbk.t[:, :], lhsT=onesf[0:1, :], rhs=brow.t[0:1, j * 512:(j + 1) * 512], start=False, stop=True),
                        reads=[brow, cst], writes=[pbk])
                    dst = g1bc if j < 2 else g2bc
                    S.op("act", lambda e, pbk=pbk, dst=dst, j=j: e.copy(out=dst.t[:, (j % 2) * 512:(j % 2 + 1) * 512], in_=pbk.t[:, :]),
                         reads=[pbk], writes=[dst])
            for j in range(2):
                S.op("dve", lambda e, j=j: e.tensor_tensor(
                    out=modfm.t[:, :, j], in0=pf[0].t[:, 0:64].rearrange("p (c j) -> p c j", j=2)[:, :, j], in1=bfm.t[:, :], op=ALU.add),
                    reads=[pf[0], bfm], writes=[modfm])
            S.op("dve", lambda e: e.scalar_tensor_tensor(out=AB.t[:, 0, :], in0=modfm.t[:, 8:16, 0], scalar=1.0, in1=n1g.t[:, :],
                                                         op0=ALU.add, op1=ALU.mult), reads=[modfm, n1g], writes=[AB])
            S.op("dve", lambda e: e.scalar_tensor_tensor(out=AB.t[:, 1, :], in0=modfm.t[:, 8:16, 1], scalar=1.0, in1=n1g.t[:, :],
                                                         op0=ALU.add, op1=ALU.mult), reads=[modfm, n1g], writes=[AB])
            S.op("dve", lambda e: e.scalar_tensor_tensor(out=AB.t[:, 2, :], in0=modfm.t[:, 24:32, 0], scalar=1.0, in1=n2g.t[:, :],
                                                         op0=ALU.add, op1=ALU.mult), reads=[modfm, n2g], writes=[AB])
            S.barrier()

        def norm_to_T(es, src_ap_fn, ntiles, tile_cfg, dstT, src_buf):
            xt = [S.sb(es, "xt%d" % i, [128, D], F32) for i in range(2)]
            junk = S.sb(es, "junk", [128, D], BF16)
            xn = [S.sb(es, "xn%d" % i, [128, D], BF16) for i in range(2)]
            ss = [S.sb(es, "ss%d" % i, [128, 2], F32) for i in range(2)]
            for t in range(ntiles):
                xb, xnb, ssb, pt = xt[t % 2], xn[t % 2], ss[t % 2], pb[t % 2]
                S.dma("sp", xb.t[:], src_ap_fn(t), reads=[src_buf], writes=[xb])
                S.op("act", lambda e, xb=xb, ssb=ssb: e.activation(out=junk.t[:], in_=xb.t[:], func=AF.Square, accum_out=ssb.t[:, 0:1]),
                     reads=[xb], writes=[junk, ssb])
                S.op("act", lambda e, ssb=ssb: e.activation(out=ssb.t[:, 1:2], in_=ssb.t[:, 0:1], func=AF.Sqrt, scale=1.0 / D, bias=epsb.t[:, 0:1]),
                     reads=[ssb, epsb], writes=[ssb])
                S.op("dve", lambda e, ssb=ssb: e.reciprocal(out=ssb.t[:, 0:1], in_=ssb.t[:, 1:2]), reads=[ssb], writes=[ssb])
                S.op("act", lambda e, xb=xb, xnb=xnb, ssb=ssb: e.activation(out=xnb.t[:], in_=xb.t[:], func=AF.Copy, scale=ssb.t[:, 0:1]),
                     reads=[xb, ssb], writes=[xnb])
                for kc in range(KC):
                    S.op("pe", lambda e, kc=kc, xnb=xnb, pt=pt: e.transpose(out=pt.t[:, kc * 128:(kc + 1) * 128],
                                                                            in_=xnb.t[:, kc * 128:(kc + 1) * 128], identity=identb),
                         reads=[xnb, cstb], writes=[pt], inc=(kc == KC - 1))
                ai, bap, tok0 = tile_cfg(t)
                for kc in range(KC):
                    S.op("dve", lambda e, kc=kc, pt=pt, ai=ai, bap=bap, tok0=tok0: e.tensor_scalar(
                        out=dstT.t[:, kc, tok0:tok0 + 128], in0=pt.t[:, kc * 128:(kc + 1) * 128],
                        scalar1=AB.t[:, ai, kc:kc + 1], scalar2=bap(kc), op0=ALU.mult, op1=ALU.add),
                        reads=[pt, AB, modfm], writes=[dstT])

        with ExitStack() as es:
            def src1(t):
                return ctx_in.t[t * 128:(t + 1) * 128, :] if t < NTC else x_in.t[(t - NTC) * 128:(t - NTC + 1) * 128, :]

            def cfg1(t):
                if t < NTC:
                    return 1, (lambda kc: modfm.t[:, kc, 1:2]), t * 128
                return 0, (lambda kc: modfm.t[:, kc, 0:1]), t * 128
            norm_to_T(es, src1, NT, cfg1, hT, x_in)
            S.barrier()
        if debug and "HTD" in debug:
            S.dma("sp", HTD.t[:, :, :], hT.t[:], reads=[hT], writes=[HTD])
        if debug and "MODD" in debug:
            S.dma("sp", MODD.t[:, 0:64], modfm.t[:].rearrange("p c j -> p (c j)"), reads=[modfm], writes=[MODD])
            S.dma("sp", MODD.t[:, 64:64 + D], g1bc.t[:], reads=[g1bc], writes=[MODD])
            S.dma("sp", MODD.t[:, 64 + D:64 + 2 * D], g2bc.t[:], reads=[g2bc], writes=[MODD])
            S.dma("sp", MODD.t[:, 64 + 2 * D:64 + 2 * D + 24], AB.t[:].rearrange("p a k -> p (a k)"), reads=[AB], writes=[MODD])


        win_v = win_in.t.rearrange("(kc p) n -> p kc n", p=128)
        with ExitStack() as es:
            cw = S.sb(es, "cw", [128, 20, 3], F32)
            cbias = S.sb(es, "cbias", [128, 20], F32)
            S.dma("sp", cw.t[:], cw_in.t[:, :, :], reads=[cw_in], writes=[cw])
            S.dma("sp", cbias.t[:], cb_in.t[:, :], reads=[cb_in], writes=[cbias])
            wg = [S.sb(es, "wg%d" % i, [128, KC, 512], BF16) for i in range(2)]
            rowbuf = [S.sb(es, "rowbuf%d" % i, [128, LT + 4], BF16) for i in range(2)]
            dg = [S.sb(es, "dg%d" % i, [128, 3, 128], BF16) for i in range(2)]
            arow = S.sb(es, "arow", [128, LT], F32)
            sqb = [S.sb(es, "sqb%d" % i, [128, 512], BF16) for i in range(2)]
            srt = [S.sb(es, "srt%d" % i, [128, 512], F32) for i in range(2)]
            ab16 = [S.sb(es, "ab16_%d" % i, [128, 512], BF16) for i in range(3)]
            tmst = [S.sb(es, "tmst%d" % i, [128, 4, 128], BF16) for i in range(2)]
            for rb in rowbuf:
                S.op("pool", lambda e, rb=rb: e.memset(rb.t[:], 0.0), writes=[rb])
            blocks = [(1, 0, LC)] + [(259 + i * 512, LC + i * 512, 512) for i in range(L // 512)]
            groups = [("q", C_Q), ("k", C_K), ("v", C_V), ("xs", C_XS), ("bc", C_BS)]
            cnt = {"ps": 0, "ab": 0, "tm": 0, "ev": 0}
            chunk_idx = 0
            for gi, (gname, c0) in enumerate(groups):
                wgb = wg[gi % 2]
                S.dma("pool", wgb.t[:], win_v[:, :, c0:c0 + 512], reads=[win_in], writes=[wgb])
                for ci in range(4):
                    cidx = {"q": 0, "k": 4, "v": 8, "xs": 12, "bc": 16}[gname] + ci
                    rb = rowbuf[chunk_idx % 2]
                    dgb = dg[chunk_idx % 2]
                    chunk_idx += 1
                    for tap in range(3):
                        S.op("dve", lambda e, dgb=dgb, tap=tap, cidx=cidx: e.tensor_scalar(
                            out=dgb.t[:, tap, :], in0=identb, scalar1=cw.t[:, cidx, tap:tap + 1], scalar2=None, op0=ALU.mult),
                            reads=[cstb, cw], writes=[dgb])
                    for (ro, t0, ntk) in blocks:
                        p = pf[cnt["ps"] % 3]
                        cnt["ps"] += 1
                        for kc in range(KC):
                            S.op("pe", lambda e, p=p, kc=kc, t0=t0, ntk=ntk, wgb=wgb, ci=ci: e.matmul(
                                p.t[:, 0:ntk], lhsT=wgb.t[:, kc, ci * 128:(ci + 1) * 128], rhs=hT.t[:, kc, t0:t0 + ntk],
                                start=(kc == 0), stop=(kc == KC - 1)), reads=[wgb, hT], writes=[p], inc=(kc == KC - 1))
                        eng = "act" if cnt["ev"] % 2 == 0 else "dve"
                        cnt["ev"] += 1
                        if eng == "act":
                            S.op("act", lambda e, p=p, rb=rb, ro=ro, ntk=ntk: e.copy(out=rb.t[:, ro:ro + ntk], in_=p.t[:, 0:ntk]),
                                 reads=[p], writes=[rb])
                        else:
                            S.op("dve", lambda e, p=p, rb=rb, ro=ro, ntk=ntk: e.tensor_copy(out=rb.t[:, ro:ro + ntk], in_=p.t[:, 0:ntk]),
                                 reads=[p], writes=[rb])
                    is_qk = gname in ("q", "k")
                    for (ro, t0, ntk) in blocks:
                        p = pf[cnt["ps"] % 3]
                        cnt["ps"] += 1
                        for tap in range(3):
                            S.op("pe", lambda e, p=p, tap=tap, ro=ro, ntk=ntk, dgb=dgb, rb=rb: e.matmul(
                                p.t[:, 0:ntk], lhsT=dgb.t[:, tap, :], rhs=rb.t[:, ro + tap - 1:ro + tap - 1 + ntk],
                                start=(tap == 0), stop=(tap == 2)), reads=[dgb, rb], writes=[p], inc=(tap == 2))
                        if is_qk:
                            S.op("act", lambda e, p=p, t0=t0, ntk=ntk: e.activation(out=arow.t[:, t0:t0 + ntk], in_=p.t[:, 0:ntk], func=AF.Silu),
                                 reads=[p], writes=[arow])
                        else:
                            a16 = ab16[cnt["ab"] % 3]
                            cnt["ab"] += 1
                            S.op("act", lambda e, p=p, ntk=ntk, a16=a16, cidx=cidx: e.activation(
                                out=a16.t[:, 0:ntk], in_=p.t[:, 0:ntk], func=AF.Silu, bias=cbias.t[:, cidx:cidx + 1]),
                                reads=[p, cbias], writes=[a16])
                            post_block(S, gname, ci, a16, t0, ntk, cnt, tmst, pb, identb, cstb,
                                       dict(QT=QT, KT=KT, Ktm=Ktm, Vtm=Vtm, Xtm=Xtm, BT=BT, CT=CT, Btm=Btm))
                    if is_qk:
                        for bi, (ro, t0, ntk) in enumerate(blocks):
                            sq = sqb[bi % 2]
                            S.op("act", lambda e, sq=sq, t0=t0, ntk=ntk: e.activation(out=sq.t[:, 0:ntk], in_=arow.t[:, t0:t0 + ntk], func=AF.Square),
                                 reads=[arow], writes=[sq])
                            p = pf[3 + bi % 2]
                            S.op("pe", lambda e, p=p, sq=sq, ntk=ntk: e.matmul(p.t[:, 0:ntk], lhsT=onesb, rhs=sq.t[:, 0:ntk], start=True, stop=True),
                                 reads=[cstb, sq], writes=[p])
                            sr = srt[bi % 2]
                            S.op("act", lambda e, p=p, sr=sr, ntk=ntk: e.activation(out=sr.t[:, 0:ntk], in_=p.t[:, 0:ntk], func=AF.Sqrt, bias=epsb.t[:, 0:1]),
                                 reads=[p, epsb], writes=[sr])
                            S.op("dve", lambda e, sr=sr, ntk=ntk: e.reciprocal(out=sr.t[:, 0:ntk], in_=sr.t[:, 0:ntk]), reads=[sr], writes=[sr])
                            a16 = ab16[cnt["ab"] % 3]
                            cnt["ab"] += 1
                            scale = (128.0 ** -0.5) if gname == "q" else 1.0
                            S.op("dve", lambda e, a16=a16, sr=sr, t0=t0, ntk=ntk, scale=scale: e.scalar_tensor_tensor(
                                out=a16.t[:, 0:ntk], in0=arow.t[:, t0:t0 + ntk], scalar=scale, in1=sr.t[:, 0:ntk], op0=ALU.mult, op1=ALU.mult),
                                reads=[arow, sr], writes=[a16])
                            post_block(S, gname, ci, a16, t0, ntk, cnt, tmst, pb, identb, cstb,
                                       dict(QT=QT, KT=KT, Ktm=Ktm, Vtm=Vtm, Xtm=Xtm, BT=BT, CT=CT, Btm=Btm))
            S.barrier()

        with ExitStack() as es:
            wtm = S.sb(es, "wtm", [128, KC, 1056], BF16)
            gcb = S.sb(es, "gcb", [128, NT, 24], F32)
            gca = S.sb(es, "gca", [128, NT, 24], F32)
            zst = [S.sb(es, "zst%d" % i, [128, 512], BF16) for i in range(3)]
            S.dma("pool", wtm.t[:, :, 0:512], win_v[:, :, C_ZG:C_ZG + 512], reads=[win_in], writes=[wtm])
            S.dma("pool", wtm.t[:, :, 512:1024], win_v[:, :, C_ZS:C_ZS + 512], reads=[win_in], writes=[wtm])
            S.dma("pool", wtm.t[:, :, 1024:1032], win_v[:, :, C_A:C_A + 8], reads=[win_in], writes=[wtm])
            S.dma("pool", wtm.t[:, :, 1032:1048], win_v[:, :, C_DT:C_DT + 16], reads=[win_in], writes=[wtm])
            S.dma("pool", wtm.t[:, :, 1048:1056], win_v[:, :, C_B:C_B + 8], reads=[win_in], writes=[wtm])
            S.dma("sp", gcb.t[:], gcb_in.t[:, :, :], reads=[gcb_in], writes=[gcb])
            S.dma("sp", gca.t[:], gca_in.t[:, :, :], reads=[gca_in], writes=[gca])
            zc = 0
            for t in range(NT):
                tk = slice(t * 128, (t + 1) * 128)
                p = pf[t % 2]
                for kc in range(KC):
                    S.op("pe", lambda e, p=p, kc=kc, tk=tk: e.matmul(p.t[:, 0:32], lhsT=hT.t[:, kc, tk], rhs=wtm.t[:, kc, 1024:1056],
                                                                     start=(kc == 0), stop=(kc == KC - 1)),
                         reads=[hT, wtm], writes=[p], inc=(kc == KC - 1))
                S.op("dve", lambda e, p=p, t=t: e.tensor_copy(out=graw.t[:, t, :], in_=p.t[:, 0:32]), reads=[p], writes=[graw])
                if t >= NTC:
                    for zi, ZD in enumerate((ZG, ZS)):
                        p2 = pf[2 + zc % 3]
                        zb = zst[zc % 3]
                        zc += 1
                        for kc in range(KC):
                            S.op("pe", lambda e, p2=p2, kc=kc, tk=tk, zi=zi: e.matmul(
                                p2.t[:, :], lhsT=hT.t[:, kc, tk], rhs=wtm.t[:, kc, zi * 512:(zi + 1) * 512],
                                start=(kc == 0), stop=(kc == KC - 1)), reads=[hT, wtm], writes=[p2], inc=(kc == KC - 1))
                        S.op("act", lambda e, p2=p2, zb=zb: e.activation(out=zb.t[:], in_=p2.t[:, :], func=AF.Silu), reads=[p2], writes=[zb])
                        S.dma("sp", ZD.t[(t - NTC) * 128:(t - NTC + 1) * 128, :], zb.t[:], reads=[zb], writes=[ZD])
            tmpg = S.sb(es, "tmpg", [128, NT, 24], F32)
            S.op("dve", lambda e: e.tensor_tensor(out=tmpg.t[:], in0=graw.t[:, :, 0:24], in1=gcb.t[:], op=ALU.add), reads=[graw, gcb], writes=[tmpg])
            S.op("act", lambda e: e.activation(out=tmpg.t[:], in_=tmpg.t[:], func=AF.Exp), reads=[tmpg], writes=[tmpg])
            S.op("act", lambda e: e.activation(out=g_sp.t[:], in_=tmpg.t[:], func=AF.Ln, bias=epsb.t[:, 2:3]), reads=[tmpg, epsb], writes=[g_sp])
            S.op("act", lambda e: e.activation(out=gca.t[:], in_=gca.t[:], func=AF.Exp), reads=[gca], writes=[gca])
            S.op("dve", lambda e: e.scalar_tensor_tensor(out=g_g.t[:], in0=g_sp.t[:], scalar=-1.0, in1=gca.t[:], op0=ALU.mult, op1=ALU.mult),
                 reads=[g_sp, gca], writes=[g_g])
            tmpb = S.sb(es, "tmpb", [128, NT, 8], F32)
            S.op("act", lambda e: e.activation(out=tmpb.t[:], in_=graw.t[:, :, 24:32], func=AF.Exp, scale=-1.0), reads=[graw], writes=[tmpb])
            S.op("dve", lambda e: e.tensor_scalar(out=tmpb.t[:], in0=tmpb.t[:], scalar1=1.0, scalar2=None, op0=ALU.add), reads=[tmpb], writes=[tmpb])
            S.op("dve", lambda e: e.reciprocal(out=g_beta.t[:], in_=tmpb.t[:]), reads=[tmpb], writes=[g_beta])
            gview = g_g.t
            pA, pB, pC = pf[0], pf[1], pf[2]
            pAv = pA.t[:, 0:NT * 8].rearrange("p (t c) -> p t c", c=8)
            pBv = pB.t[:, 0:NT * 8].rearrange("p (t c) -> p t c", c=8)
            pCv = pC.t[:, 0:NT * 8].rearrange("p (t c) -> p t c", c=8)
            S.op("pe", lambda e: e.matmul(pAv[:, :, 0:4], lhsT=trif, rhs=gview[:, :, 0:4], start=True, stop=True), reads=[cst, g_g], writes=[pA])
            S.op("pe", lambda e: e.matmul(pAv[:, :, 4:8], lhsT=trib, rhs=gview[:, :, 4:8], start=True, stop=True), reads=[cst, g_g], writes=[pA])
            S.op("pe", lambda e: e.matmul(pBv[:, :, :], lhsT=trif, rhs=gview[:, :, 8:16], start=True, stop=True), reads=[cst, g_g], writes=[pB])
            S.op("pe", lambda e: e.matmul(pCv[:, :, :], lhsT=trib, rhs=gview[:, :, 16:24], start=True, stop=True), reads=[cst, g_g], writes=[pC])
            S.op("dve", lambda e: e.tensor_copy(out=g_gc.t[:, :, 0:8], in_=pAv), reads=[pA], writes=[g_gc])
            S.op("dve", lambda e: e.tensor_copy(out=g_gc.t[:, :, 8:16], in_=pBv), reads=[pB], writes=[g_gc])
            S.op("dve", lambda e: e.tensor_copy(out=g_gc.t[:, :, 16:24], in_=pCv), reads=[pC], writes=[g_gc])
            pD, pE = pf[3], pf[4]
            gflat = g_g.t[:].rearrange("p t c -> p (t c)")
            S.op("pe", lambda e: e.matmul(pD.t[:, 0:408], lhsT=onesf, rhs=gflat[:, 0:408], start=True, stop=True), reads=[cst, g_g], writes=[pD])
            S.op("pe", lambda e: e.matmul(pE.t[:, 0:408], lhsT=onesf, rhs=gflat[:, 408:816], start=True, stop=True), reads=[cst, g_g], writes=[pE])
            tflat = g_tot.t[:].rearrange("p t c -> p (t c)")
            S.op("dve", lambda e: e.tensor_copy(out=tflat[:, 0:408], in_=pD.t[:, 0:408]), reads=[pD], writes=[g_tot])
            S.op("dve", lambda e: e.tensor_copy(out=tflat[:, 408:816], in_=pE.t[:, 0:408]), reads=[pE], writes=[g_tot])
            S.op("act", lambda e: e.activation(out=g_egc.t[:], in_=g_gc.t[:], func=AF.Exp), reads=[g_gc], writes=[g_egc])
            S.op("act", lambda e: e.activation(out=g_etot.t[:], in_=g_tot.t[:], func=AF.Exp), reads=[g_tot], writes=[g_etot])
            S.op("dve", lambda e: e.tensor_tensor(out=g_ekd.t[:], in0=g_tot.t[:], in1=g_gc.t[:], op=ALU.subtract), reads=[g_tot, g_gc], writes=[g_ekd])
            S.op("act", lambda e: e.activation(out=g_ekd.t[:], in_=g_ekd.t[:], func=AF.Exp), reads=[g_ekd], writes=[g_ekd])
            S.op("dve", lambda e: e.tensor_scalar(out=g_nbeta.t[:], in0=g_beta.t[:], scalar1=-1.0, scalar2=None, op0=ALU.mult), reads=[g_beta], writes=[g_nbeta])
            S.op("dve", lambda e: e.tensor_tensor(out=g_bek.t[:], in0=g_beta.t[:], in1=g_egc.t[:, :, 0:8], op=ALU.mult), reads=[g_beta, g_egc], writes=[g_bek])
            gcflat = g_gc.t[:].rearrange("p t c -> p (t c)")
            gct_sb = S.sb(es, "gct_sb", [128, 7, 128], F32)
            for blk in range(7):
                nr = min(128, NT * 24 - blk * 128)
                p = pf[blk % 2]
                S.op("pe", lambda e, p=p, blk=blk, nr=nr: e.transpose(out=p.t[0:nr, 0:128], in_=gcflat[:, blk * 128:blk * 128 + nr], identity=identf),
                     reads=[g_gc, cst], writes=[p])
                S.op("dve", lambda e, p=p, blk=blk, nr=nr: e.tensor_copy(out=gct_sb.t[0:nr, blk, :], in_=p.t[0:nr, 0:128]), reads=[p], writes=[gct_sb])
                S.dma("sp", GCT.t[blk * 128:blk * 128 + nr, :], gct_sb.t[0:nr, blk, :], reads=[gct_sb], writes=[GCT])
            if debug and "GD" in debug:
                for i, gb in enumerate((g_sp, g_g, g_gc, g_tot)):
                    S.dma("sp", GD.t[:, :, i * 24:(i + 1) * 24], gb.t[:], reads=[gb], writes=[GD])
                S.dma("sp", GD.t[:, :, 96:104], g_beta.t[:], reads=[g_beta], writes=[GD])
            S.barrier()


        def bc3(ap2, n):
            return ap2.unsqueeze(2).to_broadcast([128, ap2.shape[1], n])
        pbf = [Buf("pbf%d" % i, None) for i in range(2)]
        pbv = [pb[i].t[:].bitcast(F32) for i in range(2)]
        if GDN_ON:
          with ExitStack() as es:
            msk = S.sb(es, "msk", [128, 16, 128], F32)
            S.dma("sp", msk.t[:], msk_in.t[:, 0:16, :], reads=[msk_in], writes=[msk])
            kTt = [S.sb(es, "kTt%d" % i, [128, 4, 128], BF16) for i in range(2)]
            qTt = [S.sb(es, "qTt%d" % i, [128, 4, 128], BF16) for i in range(2)]
            ktmt = [S.sb(es, "ktmt%d" % i, [128, 4, 128], BF16) for i in range(2)]
            vtmt = [S.sb(es, "vtmt%d" % i, [128, 4, 128], BF16) for i in range(2)]
            rowb = [S.sb(es, "rowb%d" % i, [128, 8, 128], F32) for i in range(2)]
            eraw = S.sb(es, "eraw", [128, 8, 128], F32)
            m1 = S.sb(es, "m1", [128, 8, 128], F32)
            m2 = S.sb(es, "m2", [128, 8, 128], F32)
            er = S.sb(es, "er", [128, 8, 128], F32)
            tmpw = [S.sb(es, "tmpw%d" % i, [128, 4, 128], F32) for i in range(2)]
            Wb = [[S.sb(es, "W%d_%d" % (d, i), [128, 4, 128], F32) for i in range(2)] for d in range(2)]
            Zb = [[S.sb(es, "Z%d_%d" % (d, i), [128, 4, 128], F32) for i in range(2)] for d in range(2)]
            Pb = [[S.sb(es, "P%d_%d" % (d, i), [128, 4, 128], F32) for i in range(2)] for d in range(2)]
            PTb = [S.sb(es, "PTb%d" % d, [128, 4, 128], BF16) for d in range(2)]
            kbe = [S.sb(es, "kbe%d" % d, [128, 4, 128], BF16) for d in range(2)]
            vbb = [S.sb(es, "vbb%d" % d, [128, 4, 128], BF16) for d in range(2)]
            stq = [[S.sb(es, "stq%d_%d" % (d, i), [128, 4, 512], BF16) for i in range(2)] for d in range(2)]
            bankA = [pf[0], pf[1]]
            bankB = [pf[2], pf[3]]
            bankC = [pf[4], pf[5]]

            def v4(b):
                return b.t[:, :].rearrange("p (h k) -> p h k", h=4)
            for t in range(NT):
                tk = slice(t * 128, (t + 1) * 128)
                kT, qT, ktm, vtm, rb = kTt[t % 2], qTt[t % 2], ktmt[t % 2], vtmt[t % 2], rowb[t % 2]
                S.dma("sp", kT.t[:], KT.t[:, :, tk].rearrange("h p k -> p h k"), reads=[KT], writes=[kT])
                S.dma("sp", qT.t[:], QT.t[:, :, tk].rearrange("h p k -> p h k"), reads=[QT], writes=[qT])
                S.dma("sp", ktm.t[:].rearrange("p h k -> p (h k)"), Ktm.t[tk, :], reads=[Ktm], writes=[ktm])
                S.dma("sp", vtm.t[:].rearrange("p h k -> p (h k)"), Vtm.t[tk, :], reads=[Vtm], writes=[vtm])
                S.dma("sp", rb.t[:], GCT.t[t * 24:t * 24 + 8, :].partition_broadcast(128), reads=[GCT], writes=[rb])
                for h in range(4):
                    S.op("pe", lambda e, h=h, kT=kT: e.matmul(pbv[0][:, h * 128:(h + 1) * 128], lhsT=kT.t[:, h, :], rhs=kT.t[:, h, :], start=True, stop=True),
                         reads=[kT], writes=[pb[0]], inc=(h == 3))
                for h in range(4):
                    S.op("pe", lambda e, h=h, kT=kT, qT=qT: e.matmul(pbv[1][:, h * 128:(h + 1) * 128], lhsT=kT.t[:, h, :], rhs=qT.t[:, h, :], start=True, stop=True),
                         reads=[kT, qT], writes=[pb[1]], inc=(h == 3))
                S.op("dve", lambda e, rb=rb, t=t: e.tensor_tensor(out=eraw.t[:], in0=rb.t[:], in1=bc3(g_gc.t[:, t, 0:8], 128), op=ALU.subtract),
                     reads=[rb, g_gc], writes=[eraw])
                S.op("dve", lambda e: e.tensor_tensor(out=m1.t[:], in0=eraw.t[:], in1=msk.t[:, 0:8, :], op=ALU.add), reads=[eraw, msk], writes=[m1])
                S.op("pool", lambda e: e.tensor_tensor(out=m2.t[:], in0=msk.t[:, 8:16, :], in1=eraw.t[:], op=ALU.subtract), reads=[eraw, msk], writes=[m2])
                S.op("act", lambda e: e.activation(out=m1.t[:], in_=m1.t[:], func=AF.Exp), reads=[m1], writes=[m1])
                S.op("act", lambda e: e.activation(out=m2.t[:], in_=m2.t[:], func=AF.Exp), reads=[m2], writes=[m2])
                S.op("act", lambda e, rb=rb: e.activation(out=er.t[:], in_=rb.t[:], func=AF.Exp), reads=[rb], writes=[er])
                sq_ = [stq[d][t % 2] for d in range(2)]
                for d in range(2):
                    cs = slice(d * 4, d * 4 + 4)
                    S.op("dve", lambda e, d=d, cs=cs: e.tensor_tensor(out=tmpw[d].t[:], in0=pbv[0].rearrange("p (h k) -> p h k", h=4), in1=m2.t[:, cs, :], op=ALU.mult),
                         reads=[pb[0], m2], writes=[tmpw[d]])
                    S.op("dve", lambda e, d=d, cs=cs, t=t: e.tensor_tensor(out=Wb[d][0].t[:], in0=tmpw[d].t[:], in1=bc3(g_nbeta.t[:, t, cs], 128), op=ALU.mult),
                         reads=[tmpw[d], g_nbeta], writes=[Wb[d][0]])
                    if t >= NTC:
                        S.op("dve", lambda e, d=d, cs=cs, sq_=sq_: e.tensor_tensor(out=sq_[d].t[:, 2, :].rearrange("p (h k) -> p h k", h=4),
                                                                        in0=pbv[1].rearrange("p (h k) -> p h k", h=4), in1=m1.t[:, cs, :], op=ALU.mult),
                             reads=[pb[1], m1], writes=[sq_[d]])
                        S.op("pool", lambda e, d=d, cs=cs, qT=qT, sq_=sq_: e.tensor_tensor(out=sq_[d].t[:, 3, :].rearrange("p (h k) -> p h k", h=4),
                                                                                 in0=qT.t[:], in1=er.t[:, cs, :], op=ALU.mult),
                             reads=[qT, er], writes=[sq_[d]])
                    S.op("pool", lambda e, d=d, cs=cs, ktm=ktm, t=t: e.tensor_tensor(out=kbe[d].t[:], in0=ktm.t[:], in1=bc3(g_bek.t[:, t, cs], 128), op=ALU.mult),
                         reads=[ktm, g_bek], writes=[kbe[d]])
                    S.op("pool", lambda e, d=d, cs=cs, vtm=vtm, t=t: e.tensor_tensor(out=vbb[d].t[:], in0=vtm.t[:], in1=bc3(g_beta.t[:, t, cs], 128), op=ALU.mult),
                         reads=[vtm, g_beta], writes=[vbb[d]])
                for d in range(2):
                    for h in range(4):
                        S.op("pe", lambda e, d=d, h=h: e.transpose(out=bankB[d].t[:, h * 128:(h + 1) * 128], in_=Wb[d][0].t[:, h, :], identity=identf),
                             reads=[Wb[d][0], cst], writes=[bankB[d]], inc=(h == 3))
                for d in range(2):
                    S.op("act", lambda e, d=d: e.copy(out=Zb[d][0].t[:], in_=v4(bankB[d])), reads=[bankB[d]], writes=[Zb[d][0]])
                    S.op("dve", lambda e, d=d: e.tensor_tensor(out=Pb[d][0].t[:], in0=Zb[d][0].t[:], in1=identf.unsqueeze(1).to_broadcast([128, 4, 128]), op=ALU.add),
                         reads=[Zb[d][0], cst], writes=[Pb[d][0]])
                NSTEP = 6
                for m in range(NSTEP):
                    cur, nxt = m % 2, 1 - m % 2
                    last = (m == NSTEP - 1)
                    for d in range(2):
                        for h in range(4):
                            S.op("pe", lambda e, d=d, h=h, cur=cur: e.matmul(bankA[d].t[:, h * 128:(h + 1) * 128], lhsT=Zb[d][cur].t[:, h, :], rhs=Wb[d][cur].t[:, h, :],
                                                                            start=True, stop=True), reads=[Zb[d][cur], Wb[d][cur]], writes=[bankA[d]], inc=(h == 3))
                    if not last:
                        for d in range(2):
                            for h in range(4):
                                S.op("pe", lambda e, d=d, h=h, cur=cur: e.matmul(bankB[d].t[:, h * 128:(h + 1) * 128], lhsT=Wb[d][cur].t[:, h, :], rhs=Zb[d][cur].t[:, h, :],
                                                                                start=True, stop=True), reads=[Zb[d][cur], Wb[d][cur]], writes=[bankB[d]], inc=(h == 3))
                    for d in range(2):
                        S.op("act", lambda e, d=d, nxt=nxt: e.copy(out=Wb[d][nxt].t[:], in_=v4(bankA[d])), reads=[bankA[d]], writes=[Wb[d][nxt]])
                    if not last:
                        for d in range(2):
                            if d == 0:
                                S.op("dve", lambda e, d=d, nxt=nxt: e.tensor_copy(out=Zb[d][nxt].t[:], in_=v4(bankB[d])), reads=[bankB[d]], writes=[Zb[d][nxt]])
                            else:
                                S.op("act", lambda e, d=d, nxt=nxt: e.copy(out=Zb[d][nxt].t[:], in_=v4(bankB[d])), reads=[bankB[d]], writes=[Zb[d][nxt]])
                    for d in range(2):
                        for h in range(4):
                            S.op("pe", lambda e, d=d, h=h, cur=cur, nxt=nxt: e.matmul(bankC[d].t[:, h * 128:(h + 1) * 128], lhsT=Wb[d][nxt].t[:, h, :], rhs=Pb[d][cur].t[:, h, :],
                                                                                     start=True, stop=True), reads=[Wb[d][nxt], Pb[d][cur]], writes=[bankC[d]], inc=(h == 3))
                    for d in range(2):
                        dst = PTb[d] if last else Pb[d][nxt]
                        S.op("dve", lambda e, d=d, cur=cur, dst=dst: e.tensor_tensor(out=dst.t[:], in0=v4(bankC[d]), in1=Pb[d][cur].t[:], op=ALU.add),
                             reads=[bankC[d], Pb[d][cur]], writes=[dst])
                for d in range(2):
                    for h in range(4):
                        S.op("pe", lambda e, d=d, h=h: e.matmul(bankA[d].t[:, h * 128:(h + 1) * 128], lhsT=PTb[d].t[:, h, :], rhs=vbb[d].t[:, h, :], start=True, stop=True),
                             reads=[PTb[d], vbb[d]], writes=[bankA[d]], inc=(h == 3))
                    for h in range(4):
                        S.op("pe", lambda e, d=d, h=h: e.matmul(bankB[d].t[:, h * 128:(h + 1) * 128], lhsT=kbe[d].t[:, h, :], rhs=PTb[d].t[:, h, :], start=True, stop=True),
                             reads=[PTb[d], kbe[d]], writes=[bankB[d]], inc=(h == 3))
                for d in range(2):
                    S.op("act", lambda e, d=d, sq_=sq_: e.copy(out=sq_[d].t[:, 0, :], in_=bankA[d].t[:, :]), reads=[bankA[d]], writes=[sq_[d]])
                    S.op("dve", lambda e, d=d, sq_=sq_: e.tensor_copy(out=sq_[d].t[:, 1, :], in_=bankB[d].t[:, :]), reads=[bankB[d]], writes=[sq_[d]])
                    nk = 4 if t >= NTC else 2
                    S.dma("sp", UWQ.t[t, d, 0:nk].rearrange("k p c -> p k c"), sq_[d].t[:, 0:nk, :], reads=[sq_[d]], writes=[UWQ])
            S.barrier()

          with ExitStack() as es:
            Sf = [[S.sb(es, "Sf%d_%d" % (d, h), [128, 128], F32) for h in range(4)] for d in range(2)]
            Sh = [[S.sb(es, "Sh%d_%d" % (d, h), [128, 128], BF16) for h in range(4)] for d in range(2)]
            uw = [[S.sb(es, "uw%d_%d" % (d, i), [128, 4, 512], BF16) for i in range(3)] for d in range(2)]
            ktm2 = [[S.sb(es, "ktm2_%d_%d" % (d, i), [128, 4, 128], BF16) for i in range(3)] for d in range(2)]
            kdec = [[S.sb(es, "kdec%d_%d" % (d, i), [128, 4, 128], BF16) for i in range(2)] for d in range(2)]
            vnew = [[[S.sb(es, "vn%d_%d_%d" % (d, h, i), [128, 128], BF16) for i in range(2)] for h in range(4)] for d in range(2)]
            ost = [[S.sb(es, "ost%d_%d" % (d, i), [128, 512], F32) for i in range(2)] for d in range(2)]
            for d in range(2):
                for h in range(4):
                    S.op("pool", lambda e, d=d, h=h: e.memset(Sf[d][h].t[:], 0.0), writes=[Sf[d][h]])
                    S.op("pool", lambda e, d=d, h=h: e.memset(Sh[d][h].t[:], 0.0), writes=[Sh[d][h]])
            order_f = list(range(NT))
            order_b = [1, 0] + list(range(NT - 1, NTC - 1, -1))
            orders = [order_f, order_b]

            def loads3b(step):
                for d in range(2):
                    t = orders[d][step]
                    tk = slice(t * 128, (t + 1) * 128)
                    nk = 4 if t >= NTC else 2
                    u, k2, kd = uw[d][step % 3], ktm2[d][step % 3], kdec[d][step % 2]
                    cs = slice(d * 4, d * 4 + 4)
                    S.dma("sp", u.t[:, 0:nk, :], UWQ.t[t, d, 0:nk].rearrange("k p c -> p k c"), reads=[UWQ], writes=[u])
                    S.dma("sp", k2.t[:].rearrange("p h k -> p (h k)"), Ktm.t[tk, :], reads=[Ktm], writes=[k2])
                    S.op("pool", lambda e, kd=kd, k2=k2, t=t, cs=cs: e.tensor_tensor(out=kd.t[:], in0=k2.t[:], in1=bc3(g_ekd.t[:, t, cs], 128), op=ALU.mult),
                         reads=[k2, g_ekd], writes=[kd])
            if RUN3B:
                loads3b(0)
            for step in range(NT if RUN3B else 0):
                if step + 1 < NT:
                    loads3b(step + 1)
                units = [(d, h) for d in range(2) for h in range(4)]
                tt_ = [orders[d][step] for d in range(2)]
                for (d, h) in units:
                    u = uw[d][step % 3]
                    hs = slice(h * 128, (h + 1) * 128)
                    pW = pf[d * 2 + h // 2]
                    co = (h % 2) * 128
                    S.op("pe", lambda e, pW=pW, co=co, u=u, hs=hs, d=d, h=h: e.matmul(pW.t[:, co:co + 128], lhsT=u.t[:, 1, hs], rhs=Sh[d][h].t[:], start=True, stop=True),
                         reads=[u, Sh[d][h]], writes=[pW], inc=(h % 2 == 1))
                for (d, h) in units:
                    u = uw[d][step % 3]
                    hs = slice(h * 128, (h + 1) * 128)
                    pW = pf[d * 2 + h // 2]
                    co = (h % 2) * 128
                    vn = vnew[d][h][step % 2]
                    S.op("dve", lambda e, pW=pW, co=co, u=u, hs=hs, vn=vn: e.tensor_tensor(out=vn.t[:], in0=u.t[:, 0, hs], in1=pW.t[:, co:co + 128], op=ALU.subtract),
                         reads=[u, pW], writes=[vn])
                for (d, h) in units:
                    u = uw[d][step % 3]
                    kd = kdec[d][step % 2]
                    hs = slice(h * 128, (h + 1) * 128)
                    vn = vnew[d][h][step % 2]
                    if tt_[d] >= NTC:
                        pO = pf[4 + d]
                        S.op("pe", lambda e, pO=pO, u=u, hs=hs, d=d, h=h: e.matmul(pO.t[:, hs], lhsT=u.t[:, 3, hs], rhs=Sh[d][h].t[:], start=True, stop=False),
                             reads=[u, Sh[d][h]], writes=[pO], inc=False)
                        S.op("pe", lambda e, pO=pO, u=u, hs=hs, vn=vn: e.matmul(pO.t[:, hs], lhsT=u.t[:, 2, hs], rhs=vn.t[:], start=False, stop=True),
                             reads=[u, vn], writes=[pO], inc=False)
                    S.op("pe", lambda e, d=d, hs=hs, kd=kd, h=h, vn=vn: e.matmul(pbv[d][:, hs], lhsT=kd.t[:, h, :], rhs=vn.t[:], start=True, stop=True),
                         reads=[kd, vn], writes=[pb[d]], inc=(h == 3))
                for (d, h) in units:
                    t = tt_[d]
                    c = d * 4 + h
                    hs = slice(h * 128, (h + 1) * 128)
                    S.op("dve", lambda e, d=d, h=h, t=t, c=c, hs=hs: e.scalar_tensor_tensor(out=Sf[d][h].t[:], in0=Sf[d][h].t[:], scalar=g_etot.t[:, t, c:c + 1],
                                                                                         in1=pbv[d][:, hs], op0=ALU.mult, op1=ALU.add),
                         reads=[pb[d], Sf[d][h], g_etot], writes=[Sf[d][h]])
                    S.op("act", lambda e, d=d, h=h: e.copy(out=Sh[d][h].t[:], in_=Sf[d][h].t[:]), reads=[Sf[d][h]], writes=[Sh[d][h]])
                for d in range(2):
                    t = tt_[d]
                    if t >= NTC:
                        os_ = ost[d][step % 2]
                        S.op("act", lambda e, d=d, os_=os_: e.copy(out=os_.t[:], in_=pf[4 + d].t[:, :]), reads=[pf[4 + d]], writes=[os_])
                        OD = OF if d == 0 else OB
                        S.dma("sp", OD.t[(t - NTC) * 128:(t - NTC + 1) * 128, :], os_.t[:], reads=[os_], writes=[OD])
            S.barrier()


        if SSD_ON:
          with ExitStack() as es:
            msk16 = S.sb(es, "msk16", [128, 16, 128], F32)
            S.dma("sp", msk16.t[:], msk_in.t[:, 16:32, :], reads=[msk_in], writes=[msk16])
            Hf = [S.sb(es, "Hf%d" % d, [128, 8, 64], F32) for d in range(2)]
            Hh = [[S.sb(es, "Hh%d_%d" % (d, i), [128, 512], BF16) for i in range(2)] for d in range(2)]
            for d in range(2):
                S.op("pool", lambda e, d=d: e.memset(Hf[d].t[:], 0.0), writes=[Hf[d]])
                S.op("pool", lambda e, d=d: e.memset(Hh[d][0].t[:], 0.0), writes=[Hh[d][0]])
            BTt = [[S.sb(es, "BTt%d_%d" % (d, i), [128, 2, 128], BF16) for i in range(2)] for d in range(2)]
            CTt = [[S.sb(es, "CTt%d_%d" % (d, i), [128, 2, 128], BF16) for i in range(3)] for d in range(2)]
            btm = [[S.sb(es, "sbtm%d_%d" % (d, i), [128, 256], BF16) for i in range(2)] for d in range(2)]
            xtm = [[S.sb(es, "sxtm%d_%d" % (d, i), [128, 8, 64], BF16) for i in range(2)] for d in range(2)]
            rb8 = [[S.sb(es, "rb8_%d_%d" % (d, i), [128, 8, 128], F32) for i in range(2)] for d in range(2)]
            er8 = [S.sb(es, "er8_%d" % d, [128, 8, 128], F32) for d in range(2)]
            Mh = [S.sb(es, "Mh%d" % d, [128, 8, 128], BF16) for d in range(2)]
            xdt = [S.sb(es, "xdt%d" % d, [128, 8, 64], BF16) for d in range(2)]
            xde = [S.sb(es, "xde%d" % d, [128, 8, 64], BF16) for d in range(2)]
            ysb = [[S.sb(es, "ysb%d_%d" % (d, i), [128, 512], F32) for i in range(2)] for d in range(2)]
            stsb = [[S.sb(es, "stsb%d_%d" % (d, i), [128, 512], F32) for i in range(2)] for d in range(2)]
            yo = [S.sb(es, "yo%d" % d, [128, 8, 64], F32) for d in range(2)]
            yst = [[S.sb(es, "yst%d_%d" % (d, i), [128, 512], F32) for i in range(2)] for d in range(2)]
            order_f = list(range(NT))
            order_b = [1, 0] + list(range(NT - 1, NTC - 1, -1))
            orders = [order_f, order_b]

            def indep(step):
                for d in range(2):
                    t = orders[d][step]
                    tk = slice(t * 128, (t + 1) * 128)
                    isx = t >= NTC
                    i2 = step % 2
                    bt, ct, bm, xm, rb = BTt[d][i2], CTt[d][step % 3], btm[d][i2], xtm[d][i2], rb8[d][i2]
                    c8 = slice(8 + d * 8, 16 + d * 8)
                    if isx:
                        S.dma("sp", bt.t[:], BT.t[:, :, tk].rearrange("g p k -> p g k"), reads=[BT], writes=[bt])
                        S.dma("sp", ct.t[:], CT.t[:, :, tk].rearrange("g p k -> p g k"), reads=[CT], writes=[ct])
                        S.dma("sp", rb.t[:], GCT.t[t * 24 + 8 + d * 8:t * 24 + 16 + d * 8, :].partition_broadcast(128), reads=[GCT], writes=[rb])
                    S.dma("sp", bm.t[:], Btm.t[tk, :], reads=[Btm], writes=[bm])
                    S.dma("sp", xm.t[:].rearrange("p h k -> p (h k)"), Xtm.t[tk, :], reads=[Xtm], writes=[xm])
                    S.op("pool", lambda e, d=d, xm=xm, t=t, c8=c8: e.tensor_tensor(out=xdt[d].t[:], in0=xm.t[:], in1=bc3(g_sp.t[:, t, c8], 64), op=ALU.mult),
                         reads=[xm, g_sp], writes=[xdt[d]])
                    S.op("pool", lambda e, d=d, t=t, c8=c8: e.tensor_tensor(out=xde[d].t[:], in0=xdt[d].t[:], in1=bc3(g_ekd.t[:, t, c8], 64), op=ALU.mult),
                         reads=[xdt[d], g_ekd], writes=[xde[d]])
                    if isx:
                        for g in range(2):
                            S.op("pe", lambda e, d=d, g=g, bt=bt, ct=ct: e.matmul(pbv[d][:, g * 128:(g + 1) * 128], lhsT=bt.t[:, g, :], rhs=ct.t[:, g, :], start=True, stop=True),
                                 reads=[bt, ct], writes=[pb[d]], inc=(g == 1))
                        S.op("dve", lambda e, d=d, rb=rb, t=t, c8=c8: e.tensor_tensor(out=er8[d].t[:], in0=rb.t[:], in1=bc3(g_gc.t[:, t, c8], 128), op=ALU.subtract),
                             reads=[rb, g_gc], writes=[er8[d]])
                        S.op("pool", lambda e, d=d: e.tensor_tensor(out=er8[d].t[:], in0=er8[d].t[:], in1=msk16.t[:, d * 8:(d + 1) * 8, :], op=ALU.add),
                             reads=[er8[d], msk16], writes=[er8[d]])
                        S.op("act", lambda e, d=d: e.activation(out=er8[d].t[:], in_=er8[d].t[:], func=AF.Exp), reads=[er8[d]], writes=[er8[d]])
                        for g in range(2):
                            S.op("dve", lambda e, d=d, g=g: e.tensor_tensor(out=Mh[d].t[:, g * 4:(g + 1) * 4, :], in0=er8[d].t[:, g * 4:(g + 1) * 4, :],
                                                                           in1=pbv[d][:, g * 128:(g + 1) * 128].unsqueeze(1).to_broadcast([128, 4, 128]), op=ALU.mult),
                                 reads=[er8[d], pb[d]], writes=[Mh[d]])
                        pY = pf[d]
                        for h in range(8):
                            S.op("pe", lambda e, d=d, h=h, pY=pY: e.matmul(pY.t[:, h * 64:(h + 1) * 64], lhsT=Mh[d].t[:, h, :], rhs=xdt[d].t[:, h, :], start=True, stop=True),
                                 reads=[Mh[d], xdt[d]], writes=[pY], inc=(h == 7))
                        S.op("act", lambda e, d=d, pY=pY, i2=i2: e.copy(out=ysb[d][i2].t[:], in_=pY.t[:, :]), reads=[pY], writes=[ysb[d][i2]])
                    pSt = pf[4 + d]
                    for g in range(2):
                        S.op("pe", lambda e, d=d, g=g, pSt=pSt, bm=bm: e.matmul(pSt.t[:, g * 256:(g + 1) * 256], lhsT=bm.t[:, g * 128:(g + 1) * 128],
                                                                              rhs=xde[d].t[:, g * 4:(g + 1) * 4, :].rearrange("p h k -> p (h k)"), start=True, stop=True),
                             reads=[bm, xde[d]], writes=[pSt], inc=(g == 1))
                    S.op("act", lambda e, d=d, pSt=pSt, i2=i2: e.copy(out=stsb[d][i2].t[:], in_=pSt.t[:, :]), reads=[pSt], writes=[stsb[d][i2]])

            def dep(step):
                for d in range(2):
                    t = orders[d][step]
                    isx = t >= NTC
                    i2 = step % 2
                    ct = CTt[d][step % 3]
                    c8 = slice(8 + d * 8, 16 + d * 8)
                    hcur, hnxt = Hh[d][step % 2], Hh[d][(step + 1) % 2]
                    if isx:
                        pO = pf[2 + d]
                        for g in range(2):
                            S.op("pe", lambda e, d=d, g=g, pO=pO, ct=ct, hcur=hcur: e.matmul(pO.t[:, g * 256:(g + 1) * 256], lhsT=ct.t[:, g, :], rhs=hcur.t[:, g * 256:(g + 1) * 256],
                                                                                           start=True, stop=True), reads=[ct, hcur], writes=[pO], inc=(g == 1))
                    S.op("dve", lambda e, d=d, t=t, c8=c8: e.tensor_tensor(out=Hf[d].t[:], in0=Hf[d].t[:], in1=bc3(g_etot.t[:, t, c8], 64), op=ALU.mult),
                         reads=[Hf[d], g_etot], writes=[Hf[d]])
                    S.op("dve", lambda e, d=d, i2=i2: e.tensor_tensor(out=Hf[d].t[:].rearrange("p h k -> p (h k)"), in0=Hf[d].t[:].rearrange("p h k -> p (h k)"),
                                                                     in1=stsb[d][i2].t[:], op=ALU.add), reads=[Hf[d], stsb[d][i2]], writes=[Hf[d]])
                    S.op("act", lambda e, d=d, hnxt=hnxt: e.copy(out=hnxt.t[:], in_=Hf[d].t[:].rearrange("p h k -> p (h k)")), reads=[Hf[d]], writes=[hnxt])
                    if isx:
                        ys = yst[d][i2]
                        S.op("dve", lambda e, d=d, pO=pO, t=t, c8=c8: e.tensor_tensor(out=yo[d].t[:], in0=pO.t[:, :].rearrange("p (h k) -> p h k", h=8),
                                                                                   in1=bc3(g_egc.t[:, t, c8], 64), op=ALU.mult),
                             reads=[pO, g_egc], writes=[yo[d]])
                        S.op("pool", lambda e, d=d, ys=ys, i2=i2: e.tensor_tensor(out=ys.t[:], in0=yo[d].t[:].rearrange("p h k -> p (h k)"), in1=ysb[d][i2].t[:], op=ALU.add),
                             reads=[yo[d], ysb[d][i2]], writes=[ys])
                        YD = YF if d == 0 else YB
                        S.dma("sp", YD.t[(t - NTC) * 128:(t - NTC + 1) * 128, :], ys.t[:], reads=[ys], writes=[YD])
            indep(0)
            for step in range(NT):
                if step + 1 < NT:
                    indep(step + 1)
                dep(step)
            S.barrier()


        wout_v = wout_in.t.rearrange("(kc p) n -> p kc n", p=128)
        with ExitStack() as es:
            wo = S.sb(es, "wo", [128, KC, D], BF16)
            for hf in range(2):
                S.dma("pool", wo.t[:, :, hf * 512:(hf + 1) * 512], wout_v[:, :, hf * 512:(hf + 1) * 512], reads=[wout_in], writes=[wo])
            nrm = S.sb(es, "nrmsb", [128, 3, 512], F32)
            S.dma("sp", nrm.t[:], nrm_in.t[:, :, :], reads=[nrm_in], writes=[nrm])
            of_ = [S.sb(es, "of%d" % i, [128, 512], F32) for i in range(2)]
            ob_ = [S.sb(es, "ob%d" % i, [128, 512], F32) for i in range(2)]
            yf_ = [S.sb(es, "yf%d" % i, [128, 512], F32) for i in range(2)]
            yb_ = [S.sb(es, "yb%d" % i, [128, 512], F32) for i in range(2)]
            zg_ = [S.sb(es, "zg%d" % i, [128, 512], BF16) for i in range(2)]
            zs_ = [S.sb(es, "zs%d" % i, [128, 512], BF16) for i in range(2)]
            xs_ = [S.sb(es, "xs%d" % i, [128, 512], BF16) for i in range(2)]
            xr_ = [S.sb(es, "xr%d" % i, [128, D], F32) for i in range(2)]
            junk5 = S.sb(es, "junk5", [128, 512], BF16)
            st5 = [S.sb(es, "st5_%d" % i, [128, 8], F32) for i in range(2)]
            mixb = [S.sb(es, "mixb%d" % i, [128, D], BF16) for i in range(2)]
            mixT = [S.sb(es, "mixT%d" % i, [128, KC, 128], BF16) for i in range(2)]
            x1t = [S.sb(es, "x1t%d" % i, [128, D], F32) for i in range(2)]
            for tt in range(L // 128):
                i2 = tt % 2
                rs = slice(tt * 128, (tt + 1) * 128)
                of, ob, yf, yb, zg, zs, xs, xr, st, mb, mT, x1 = of_[i2], ob_[i2], yf_[i2], yb_[i2], zg_[i2], zs_[i2], xs_[i2], xr_[i2], st5[i2], mixb[i2], mixT[i2], x1t[i2]
                S.dma("sp", of.t[:], OF.t[rs, :], reads=[OF], writes=[of])
                S.dma("sp", ob.t[:], OB.t[rs, :], reads=[OB], writes=[ob])
                S.dma("sp", yf.t[:], YF.t[rs, :], reads=[YF], writes=[yf])
                S.dma("sp", yb.t[:], YB.t[rs, :], reads=[YB], writes=[yb])
                S.dma("sp", zg.t[:], ZG.t[rs, :], reads=[ZG], writes=[zg])
                S.dma("sp", zs.t[:], ZS.t[rs, :], reads=[ZS], writes=[zs])
                S.dma("sp", xs.t[:], Xtm.t[LC + tt * 128:LC + (tt + 1) * 128, :], reads=[Xtm], writes=[xs])
                S.dma("sp", xr.t[:], x_in.t[rs, :], reads=[x_in], writes=[xr])
                S.op("dve", lambda e, of=of, ob=ob: e.tensor_tensor(out=of.t[:], in0=of.t[:], in1=ob.t[:], op=ALU.add), reads=[of, ob], writes=[of])
                for h in range(4):
                    S.op("act", lambda e, of=of, st=st, h=h: e.activation(out=junk5.t[:, 0:128], in_=of.t[:, h * 128:(h + 1) * 128], func=AF.Square, accum_out=st.t[:, h:h + 1]),
                         reads=[of], writes=[junk5, st])
                S.op("act", lambda e, st=st: e.activation(out=st.t[:, 0:4], in_=st.t[:, 0:4], func=AF.Sqrt, scale=1.0 / 128, bias=epsb.t[:, 0:1]), reads=[st, epsb], writes=[st])
                S.op("dve", lambda e, st=st: e.reciprocal(out=st.t[:, 0:4], in_=st.t[:, 0:4]), reads=[st], writes=[st])
                S.op("dve", lambda e, of=of, st=st: e.tensor_tensor(out=of.t[:].rearrange("p (h k) -> p h k", h=4), in0=of.t[:].rearrange("p (h k) -> p h k", h=4),
                                                                 in1=bc3(st.t[:, 0:4], 128), op=ALU.mult), reads=[of, st], writes=[of])
                S.op("dve", lambda e, of=of: e.tensor_tensor(out=of.t[:], in0=of.t[:], in1=nrm.t[:, 0, :], op=ALU.mult), reads=[of, nrm], writes=[of])
                S.op("dve", lambda e, of=of, zg=zg, mb=mb: e.tensor_tensor(out=mb.t[:, 0:512], in0=of.t[:], in1=zg.t[:], op=ALU.mult), reads=[of, zg], writes=[mb])
                S.op("pool", lambda e, yf=yf, yb=yb: e.tensor_tensor(out=yf.t[:], in0=yf.t[:], in1=yb.t[:], op=ALU.add), reads=[yf, yb], writes=[yf])
                S.op("pool", lambda e, yb=yb, xs=xs: e.tensor_tensor(out=yb.t[:], in0=xs.t[:], in1=nrm.t[:, 2, :], op=ALU.mult), reads=[xs, nrm], writes=[yb])
                S.op("pool", lambda e, yf=yf, yb=yb: e.tensor_tensor(out=yf.t[:], in0=yf.t[:], in1=yb.t[:], op=ALU.add), reads=[yf, yb], writes=[yf])
                S.op("pool", lambda e, yf=yf, zs=zs: e.tensor_tensor(out=yf.t[:], in0=yf.t[:], in1=zs.t[:], op=ALU.mult), reads=[yf, zs], writes=[yf])
                S.op("act", lambda e, yf=yf, st=st: e.activation(out=junk5.t[:], in_=yf.t[:], func=AF.Square, accum_out=st.t[:, 4:5]), reads=[yf], writes=[junk5, st])
                S.op("act", lambda e, st=st: e.activation(out=st.t[:, 4:5], in_=st.t[:, 4:5], func=AF.Sqrt, scale=1.0 / 512, bias=epsb.t[:, 0:1]), reads=[st, epsb], writes=[st])
                S.op("dve", lambda e, st=st: e.reciprocal(out=st.t[:, 4:5], in_=st.t[:, 4:5]), reads=[st], writes=[st])
                S.op("dve", lambda e, yf=yf, st=st, mb=mb: e.scalar_tensor_tensor(out=mb.t[:, 512:1024], in0=yf.t[:], scalar=st.t[:, 4:5], in1=nrm.t[:, 1, :], op0=ALU.mult, op1=ALU.mult),
                     reads=[yf, st, nrm], writes=[mb])
                pt = pb[i2]
                for kc in range(KC):
                    S.op("pe", lambda e, kc=kc, mb=mb, pt=pt: e.transpose(out=pt.t[:, kc * 128:(kc + 1) * 128], in_=mb.t[:, kc * 128:(kc + 1) * 128], identity=identb),
                         reads=[mb, cstb], writes=[pt], inc=(kc == KC - 1))
                S.op("act", lambda e, pt=pt, mT=mT: e.copy(out=mT.t[:].rearrange("p a b -> p (a b)"), in_=pt.t[:, :]), reads=[pt], writes=[mT])
                for hf in range(2):
                    p = pf[(tt * 2 + hf) % 4]
                    for kc in range(KC):
                        S.op("pe", lambda e, p=p, kc=kc, mT=mT, hf=hf: e.matmul(p.t[:, :], lhsT=mT.t[:, kc, :], rhs=wo.t[:, kc, hf * 512:(hf + 1) * 512],
                                                                              start=(kc == 0), stop=(kc == KC - 1)), reads=[mT, wo], writes=[p], inc=(kc == KC - 1))
                    S.op("dve", lambda e, p=p, x1=x1, hf=hf: e.tensor_tensor(out=x1.t[:, hf * 512:(hf + 1) * 512], in0=p.t[:, :], in1=g1bc.t[:, hf * 512:(hf + 1) * 512], op=ALU.mult),
                         reads=[p, g1bc], writes=[x1])
                S.op("pool", lambda e, x1=x1, xr=xr: e.tensor_tensor(out=x1.t[:], in0=x1.t[:], in1=xr.t[:], op=ALU.add), reads=[x1, xr], writes=[x1])
                S.dma("sp", X1.t[rs, :], x1.t[:], reads=[x1], writes=[X1])
            S.barrier()

        with ExitStack() as es:
            def src2(t):
                return X1.t[t * 128:(t + 1) * 128, :]

            def cfg2(t):
                return 2, (lambda kc: modfm.t[:, 16 + kc, 0:1]), t * 128
            norm_to_T(es, src2, L // 128, cfg2, hT, X1)
            S.barrier()
        wg_v = wgate_in.t.rearrange("(kc p) n -> p kc n", p=128)
        wu_v = wup_in.t.rearrange("(kc p) n -> p kc n", p=128)
        with ExitStack() as es:
            fcw = S.sb(es, "fcw", [128, FC, 9], F32)
            fcb = S.sb(es, "fcb", [128, FC], F32)
            S.dma("sp", fcw.t[:], fcw_in.t[:, :, :], reads=[fcw_in], writes=[fcw])
            S.dma("sp", fcb.t[:], fcb_in.t[:, :], reads=[fcb_in], writes=[fcb])
            wgt = [S.sb(es, "wgt%d" % i, [128, KC, 128], BF16) for i in range(2)]
            wut = [S.sb(es, "wut%d" % i, [128, KC, 128], BF16) for i in range(2)]
            gimg = [S.sb(es, "gimg%d" % i, [128, 66, 66], BF16) for i in range(2)]
            dg9 = [S.sb(es, "dg9_%d" % i, [128, 9, 128], BF16) for i in range(2)]
            asil = [S.sb(es, "asil%d" % i, [128, 512], F32) for i in range(2)]
            midb = [S.sb(es, "midb%d" % i, [128, 512], BF16) for i in range(3)]
            for gb in gimg:
                S.op("pool", lambda e, gb=gb: e.memset(gb.t[:], 0.0), writes=[gb])
            mc = 0
            for fc in range(FC):
                i2 = fc % 2
                wgb, wub, gb, d9 = wgt[i2], wut[i2], gimg[i2], dg9[i2]
                S.dma("pool", wgb.t[:], wg_v[:, :, fc * 128:(fc + 1) * 128], reads=[wgate_in], writes=[wgb])
                S.dma("pool", wub.t[:], wu_v[:, :, fc * 128:(fc + 1) * 128], reads=[wup_in], writes=[wub])
                for tap in range(9):
                    S.op("dve", lambda e, d9=d9, tap=tap, fc=fc: e.tensor_scalar(out=d9.t[:, tap, :], in0=identb, scalar1=fcw.t[:, fc, tap:tap + 1], scalar2=None, op0=ALU.mult),
                         reads=[cstb, fcw], writes=[d9])
                for b in range(8):
                    p = pf[b % 2]
                    for kc in range(KC):
                        S.op("pe", lambda e, p=p, kc=kc, b=b, wgb=wgb: e.matmul(p.t[:, :], lhsT=wgb.t[:, kc, :], rhs=hT.t[:, kc, b * 512:(b + 1) * 512],
                                                                              start=(kc == 0), stop=(kc == KC - 1)), reads=[wgb, hT], writes=[p], inc=(kc == KC - 1))
                    S.op("act", lambda e, p=p, gb=gb, b=b: e.copy(out=gb.t[:, 1 + 8 * b:9 + 8 * b, 1:65], in_=p.t[:, :].rearrange("p (r c) -> p r c", c=64)),
                         reads=[p], writes=[gb])
                for b in range(8):
                    p = pf[2 + b % 2]
                    for tap in range(9):
                        dr, dc = tap // 3, tap % 3
                        S.op("pe", lambda e, p=p, tap=tap, dr=dr, dc=dc, b=b, d9=d9, gb=gb: e.matmul(
                            p.t[:, :].rearrange("p (r c) -> p r c", c=64), lhsT=d9.t[:, tap, :], rhs=gb.t[:, 8 * b + dr:8 * b + dr + 8, dc:dc + 64],
                            start=(tap == 0), stop=(tap == 8)), reads=[d9, gb], writes=[p], inc=(tap == 8))
                    a = asil[b % 2]
                    S.op("act", lambda e, p=p, a=a, fc=fc: e.activation(out=a.t[:], in_=p.t[:, :], func=AF.Silu, bias=fcb.t[:, fc:fc + 1]), reads=[p, fcb], writes=[a])
                    p2 = pf[4 + b % 2]
                    for kc in range(KC):
                        S.op("pe", lambda e, p2=p2, kc=kc, b=b, wub=wub: e.matmul(p2.t[:, :], lhsT=wub.t[:, kc, :], rhs=hT.t[:, kc, b * 512:(b + 1) * 512],
                                                                                start=(kc == 0), stop=(kc == KC - 1)), reads=[wub, hT], writes=[p2], inc=(kc == KC - 1))
                    mb = midb[mc % 3]
                    mc += 1
                    S.op("dve", lambda e, p2=p2, a=a, mb=mb: e.tensor_tensor(out=mb.t[:], in0=a.t[:], in1=p2.t[:, :], op=ALU.mult), reads=[a, p2], writes=[mb])
                    S.dma("sp", MID.t[fc, :, b * 512:(b + 1) * 512], mb.t[:], reads=[mb], writes=[MID])
            S.barrier()
        wd_v = wdown_in.t.rearrange("(fc p) n -> p fc n", p=128)
        with ExitStack() as es:
            wd = S.sb(es, "wd", [128, FC, D], BF16)
            for hf in range(2):
                for q4 in range(0, FC, 2):
                    S.dma("pool", wd.t[:, q4:q4 + 2, hf * 512:(hf + 1) * 512], wd_v[:, q4:q4 + 2, hf * 512:(hf + 1) * 512], reads=[wdown_in], writes=[wd])
            fng = S.sb(es, "fng", [128, D], F32)
            S.dma("sp", fng.t[:], fng_in.t[:, :], reads=[fng_in], writes=[fng])
            midt = [S.sb(es, "midt%d" % i, [128, FC, 256], BF16) for i in range(2)]
            x1r = [S.sb(es, "x1r%d" % i, [128, D], F32) for i in range(2)]
            x2t = [S.sb(es, "x2t%d" % i, [128, D], F32) for i in range(2)]
            junk6 = S.sb(es, "junk6", [128, D], BF16)
            st6 = [S.sb(es, "st6_%d" % i, [128, 2], F32) for i in range(2)]
            for tt in range(L // 128):
                i2 = tt % 2
                rs = slice(tt * 128, (tt + 1) * 128)
                blk, j = tt // 2, tt % 2
                mt, xr, x2, st = midt[blk % 2], x1r[i2], x2t[i2], st6[i2]
                if j == 0:
                    for q4 in range(0, FC, 11):
                        S.dma("sp", mt.t[:, q4:q4 + 11, :], MID.t[q4:q4 + 11, :, blk * 256:(blk + 1) * 256].rearrange("f p k -> p f k"), reads=[MID], writes=[mt])
                S.dma("sp", xr.t[:], X1.t[rs, :], reads=[X1], writes=[xr])
                for hf in range(2):
                    p = pf[(tt * 2 + hf) % 4]
                    for fc in range(FC):
                        S.op("pe", lambda e, p=p, fc=fc, mt=mt, hf=hf, j=j: e.matmul(p.t[:, :], lhsT=mt.t[:, fc, j * 128:(j + 1) * 128], rhs=wd.t[:, fc, hf * 512:(hf + 1) * 512],
                                                                                   start=(fc == 0), stop=(fc == FC - 1)), reads=[mt, wd], writes=[p], inc=(fc == FC - 1))
                    S.op("dve", lambda e, p=p, x2=x2, hf=hf: e.tensor_tensor(out=x2.t[:, hf * 512:(hf + 1) * 512], in0=p.t[:, :], in1=g2bc.t[:, hf * 512:(hf + 1) * 512], op=ALU.mult),
                         reads=[p, g2bc], writes=[x2])
                S.op("pool", lambda e, x2=x2, xr=xr: e.tensor_tensor(out=x2.t[:], in0=x2.t[:], in1=xr.t[:], op=ALU.add), reads=[x2, xr], writes=[x2])
                S.op("act", lambda e, x2=x2, st=st: e.activation(out=junk6.t[:], in_=x2.t[:], func=AF.Square, accum_out=st.t[:, 0:1]), reads=[x2], writes=[junk6, st])
                S.op("act", lambda e, st=st: e.activation(out=st.t[:, 1:2], in_=st.t[:, 0:1], func=AF.Sqrt, scale=1.0 / D, bias=epsb.t[:, 0:1]), reads=[st, epsb], writes=[st])
                S.op("dve", lambda e, st=st: e.reciprocal(out=st.t[:, 0:1], in_=st.t[:, 1:2]), reads=[st], writes=[st])
                S.op("dve", lambda e, x2=x2, st=st: e.scalar_tensor_tensor(out=x2.t[:], in0=x2.t[:], scalar=st.t[:, 0:1], in1=fng.t[:], op0=ALU.mult, op1=ALU.mult),
                     reads=[x2, st, fng], writes=[x2])
                S.dma("sp", y_out.t[rs, :], x2.t[:], reads=[x2], writes=[y_out])
            S.barrier()


        S.barrier()

        with nc.Block() as block:
            @block.sync
            def _(eng):
                S.replay("sp", eng)

            @block.tensor
            def _(eng):
                S.replay("pe", eng)

            @block.vector
            def _(eng):
                S.replay("dve", eng)

            @block.scalar
            def _(eng):
                S.replay("act", eng)

            @block.gpsimd
            def _(eng):
                S.replay("pool", eng)
        print("n_ops", S.n_ops, {e: len(S.streams[e]) for e in ENGS})
    return nc


def host_inputs(inputs, b):
    f = np.float32
    c = np.asarray(inputs["c"][b], f)
    cc = np.asarray(inputs["c_ctx"], f)
    cT = np.stack([c.reshape(KC, 128).T, cc.reshape(KC, 128).T], axis=-1)
    b_ada = np.asarray(inputs["b_ada"][0], f)
    bch = b_ada.reshape(48, 128)
    sel = list(range(0, 16)) + list(range(24, 40))
    b_fm = bch[sel].T.copy()
    b_row = np.concatenate([b_ada[2048:3072], b_ada[5120:6144]])[None, :]
    gcw = np.asarray(inputs["gdn_conv_w"][0], f)
    scw = np.asarray(inputs["ssm_conv_w"][0], f)
    cw = np.concatenate([gcw, scw], axis=1)
    cw_fm = cw.reshape(3, 20, 128).transpose(2, 1, 0).copy()
    cb = np.concatenate([np.zeros(1536, f), np.asarray(inputs["ssm_conv_b"][0], f)])
    cb_fm = cb.reshape(20, 128).T.copy()
    ident = np.eye(128, dtype=f)
    j = np.arange(128)[:, None]
    i = np.arange(128)[None, :]
    trif = (j <= i).astype(f)
    trib = (j >= i).astype(f)
    ones = np.ones((128, 128), f)
    consts = np.concatenate([ident, trif, trib, ones, np.zeros((128, 512), f)], axis=1)
    gb = np.concatenate([np.asarray(inputs["gdn_dt_bias"][0], f).reshape(8), np.asarray(inputs["ssm_dt_bias"][0], f).reshape(16)])
    ga = np.concatenate([np.asarray(inputs["gdn_A_log"][0], f).reshape(8), np.asarray(inputs["ssm_A_log"][0], f).reshape(16)])
    NEG = np.float32(-1.0e4)
    pq = np.arange(128)[:, None]
    fq = np.arange(128)[None, :]
    mq_f = np.where(pq <= fq, 0.0, NEG).astype(f)
    mq_b = np.where(pq >= fq, 0.0, NEG).astype(f)
    mk_f = np.where(fq < pq, 0.0, NEG).astype(f)
    mk_b = np.where(fq > pq, 0.0, NEG).astype(f)
    masks = np.stack([mq_f] * 4 + [mq_b] * 4 + [mk_f] * 4 + [mk_b] * 4 + [mq_f] * 8 + [mq_b] * 8, axis=1)
    gate_bias = np.broadcast_to(gb[None, None, :], (128, NT, 24)).copy()
    gate_alog = np.broadcast_to(ga[None, None, :], (128, NT, 24)).copy()
    return {
        "x": np.ascontiguousarray(inputs["x"][b], f),
        "ctx": np.ascontiguousarray(inputs["ctx"][b], f),
        "cT": np.ascontiguousarray(cT, f),
        "w_ada": np.ascontiguousarray(inputs["w_ada"][0], f),
        "b_ada_fm": b_fm, "b_ada_row": np.ascontiguousarray(b_row, f),
        "n1g_fm": np.asarray(inputs["norm1_g"][0], f).reshape(KC, 128).T.copy(),
        "n2g_fm": np.asarray(inputs["norm2_g"][0], f).reshape(KC, 128).T.copy(),
        "w_in": np.ascontiguousarray(inputs["w_in"][0], f),
        "convw_fm": cw_fm, "convb_fm": cb_fm, "consts": consts,
        "gate_bias": gate_bias, "gate_alog": gate_alog, "masks": np.ascontiguousarray(masks, f),
        "w_out": np.ascontiguousarray(inputs["w_out"][0], f),
        "nrm": np.ascontiguousarray(np.broadcast_to(np.stack([np.tile(np.asarray(inputs["gdn_norm_g"][0], f), 4), np.asarray(inputs["ssm_norm_g"][0], f),
                                                              np.repeat(np.asarray(inputs["ssm_D"][0], f), 64)])[None], (128, 3, 512)), f),
        "w_gate": np.ascontiguousarray(inputs["ffn_w_gate"][0], f), "w_up": np.ascontiguousarray(inputs["ffn_w_up"][0], f),
        "w_down": np.ascontiguousarray(inputs["ffn_w_down"][0], f),
        "fcw_fm": np.ascontiguousarray(np.asarray(inputs["ffn_conv_w"][0], f).reshape(9, FC, 128).transpose(2, 1, 0), f),
        "fcb_fm": np.ascontiguousarray(np.asarray(inputs["ffn_conv_b"][0], f).reshape(FC, 128).T, f),
        "fng_bc": np.ascontiguousarray(np.broadcast_to(np.asarray(inputs["final_norm_g"], f)[None], (128, D)), f),
    }


def kernel(**inputs):
    nc = build()
    in_maps = [host_inputs(inputs, b) for b in range(8)]
    res = run_bass_kernel_spmd(nc, in_maps, core_ids=list(range(8)))
    return np.stack([np.asarray(r["y"], np.float32) for r in res.results], axis=0)
```
